# Optimizing a Trainium2 kernel written in Bass

```python
import jax, jax.numpy as jnp
from jax import lax
import numpy as np

D_MODEL = 1024
BATCH = 8
SEQ = 2048
DEPTH = 4
DEC_BATCH = 128
DEC_SEQ = 4
PAST_LEN = 8192
PAGE_SIZE = 128

N_AB = (DEPTH + 1) // 2
N_C = DEPTH // 2
GLA_HEADS = 4
GLA_DK = 64
GLA_DV = 128
GLA_RANK = 16
GLA_TAU = 16.0
GLA_CHUNK = 64
MLA_HEADS = 8
MLA_Q_LORA = 256
MLA_KV_LORA = 256
MLA_NOPE = 64
MLA_ROPE = 32
MLA_V = 64
MLA_SCALE = (MLA_NOPE + MLA_ROPE) ** -0.5
ROPE_THETA = 10000.0
Q_BLOCK = 128
CONV_W = 31
D_FF = 4 * D_MODEL
EPS = 1e-6

GLA_QK_W = GLA_HEADS * GLA_DK
GLA_V_W = GLA_HEADS * GLA_DV
MLA_OUT_W = MLA_HEADS * MLA_V
D_MIX_OUT = GLA_V_W + MLA_OUT_W
SPLIT_POINTS = (GLA_QK_W,
                2 * GLA_QK_W,
                2 * GLA_QK_W + GLA_V_W,
                2 * GLA_QK_W + GLA_V_W + GLA_RANK,
                2 * GLA_QK_W + 2 * GLA_V_W + GLA_RANK,
                2 * GLA_QK_W + 2 * GLA_V_W + GLA_RANK + MLA_Q_LORA)
D_IN = SPLIT_POINTS[-1] + MLA_KV_LORA + MLA_ROPE

kernel_name = 'hybrid_gla_mla_conformer_decode_step'


def rmsnorm(x, g):
    xf = x.astype(jnp.float32)
    y = xf * lax.rsqrt(jnp.mean(xf * xf, axis=-1, keepdims=True) + EPS)
    return (y * g.astype(jnp.float32)).astype(x.dtype)


def layernorm(x, g, b):
    xf = x.astype(jnp.float32)
    mu = jnp.mean(xf, axis=-1, keepdims=True)
    var = jnp.mean(jnp.square(xf - mu), axis=-1, keepdims=True)
    y = (xf - mu) * lax.rsqrt(var + EPS)
    return (y * g.astype(jnp.float32) + b.astype(jnp.float32)).astype(x.dtype)


def rope_table(pos):
    inv = jnp.power(ROPE_THETA, -jnp.arange(0, MLA_ROPE, 2, dtype=jnp.float32) / MLA_ROPE)
    ang = pos.astype(jnp.float32)[:, None] * inv[None, :]
    ang = jnp.concatenate([ang, ang], axis=-1)
    return jnp.cos(ang), jnp.sin(ang)


def apply_rope(x, cos, sin):
    half = x.shape[-1] // 2
    rot = jnp.concatenate([-x[..., half:], x[..., :half]], axis=-1)
    return (x * cos + rot * sin).astype(x.dtype)


def gla_recurrence(q, k, v, log_a, s0):
    f32 = jnp.float32
    B, T = q.shape[0], q.shape[1]
    c = min(GLA_CHUNK, T)
    n = -(-T // c)
    pad = n * c - T

    def prep(z):
        z = jnp.pad(z.astype(f32), ((0, 0), (0, pad), (0, 0), (0, 0)))
        return z.reshape(B, n, c, z.shape[2], z.shape[3]).transpose(1, 0, 3, 2, 4)

    qc, kc, vc, gc = prep(q), prep(k), prep(v), prep(log_a)
    causal = jnp.tril(jnp.ones((c, c), dtype=bool))[:, :, None]

    def step(s, inp):
        qi, ki, vi, gi = inp
        b = jnp.cumsum(gi, axis=2)
        o_inter = jnp.einsum('bhtd,bhde->bhte', qi * jnp.exp(b), s)
        diff = b[:, :, :, None, :] - b[:, :, None, :, :]
        decay = jnp.exp(jnp.where(causal, diff, -jnp.inf))
        attn = jnp.einsum('bhtd,bhsd,bhtsd->bhts', qi, ki, decay)
        o = o_inter + jnp.einsum('bhts,bhse->bhte', attn, vi)
        b_last = b[:, :, -1:, :]
        s_new = (jnp.exp(b_last[:, :, 0, :])[..., None] * s
                 + jnp.einsum('bhsd,bhse->bhde', ki * jnp.exp(b_last - b), vi))
        return s_new, o

    s_final, oc = lax.scan(step, s0.astype(f32), (qc, kc, vc, gc))
    o = oc.transpose(1, 0, 3, 2, 4).reshape(B, n * c, q.shape[2], v.shape[3])[:, :T]
    return o, s_final


def mla_self(q_nope, q_pe, c_kv, k_pe, w_ukv):
    B, S = c_kv.shape[0], c_kv.shape[1]
    kv = (c_kv @ w_ukv).reshape(B, S, MLA_HEADS, MLA_NOPE + MLA_V)
    k_nope, v = kv[..., :MLA_NOPE], kv[..., MLA_NOPE:]
    qb = min(Q_BLOCK, S)
    nb = S // qb
    key_pos = jnp.arange(S)

    def block(i):
        start = i * qb
        qn = lax.dynamic_slice_in_dim(q_nope, start, qb, axis=1)
        qp = lax.dynamic_slice_in_dim(q_pe, start, qb, axis=1)
        sc = jnp.einsum('bthn,bshn->bhts', qn, k_nope) + jnp.einsum('bthr,bsr->bhts', qp, k_pe)
        sc = sc.astype(jnp.float32) * MLA_SCALE
        mask = (start + jnp.arange(qb))[:, None] >= key_pos[None, :]
        p = jax.nn.softmax(jnp.where(mask, sc, -jnp.inf), axis=-1).astype(v.dtype)
        return jnp.einsum('bhts,bshv->bthv', p, v)

    o = lax.map(block, jnp.arange(nb))
    return o.transpose(1, 0, 2, 3, 4).reshape(B, S, MLA_HEADS * MLA_V)


def mla_cached(q_nope, q_pe, c_new, kr_new, past_c, past_r, w_ukv):
    B, T = c_new.shape[0], c_new.shape[1]
    P = past_c.shape[1]
    w = w_ukv.reshape(MLA_KV_LORA, MLA_HEADS, MLA_NOPE + MLA_V)
    w_uk, w_uv = w[..., :MLA_NOPE], w[..., MLA_NOPE:]
    q_lat = jnp.einsum('bthn,chn->bthc', q_nope, w_uk)
    s_past = jnp.einsum('bthc,bpc->bhtp', q_lat, past_c) + jnp.einsum('bthr,bpr->bhtp', q_pe, past_r)
    s_new = jnp.einsum('bthc,bsc->bhts', q_lat, c_new) + jnp.einsum('bthr,bsr->bhts', q_pe, kr_new)
    causal = jnp.tril(jnp.ones((T, T), dtype=bool))
    s_new = jnp.where(causal, s_new.astype(jnp.float32), -jnp.inf)
    sc = jnp.concatenate([s_past.astype(jnp.float32), s_new], axis=-1) * MLA_SCALE
    p = jax.nn.softmax(sc, axis=-1).astype(c_new.dtype)
    o_lat = (jnp.einsum('bhtp,bpc->bthc', p[..., :P], past_c)
             + jnp.einsum('bhts,bsc->bthc', p[..., P:], c_new))
    o = jnp.einsum('bthc,chv->bthv', o_lat, w_uv)
    return o.reshape(B, T, MLA_HEADS * MLA_V)


def ab_mixer(h, gla_s0, cos, sin, past_c, past_r, w_in, w_gate_a2, b_gate_a, gla_norm,
             q_norm, kv_norm, w_uq, w_ukv, w_out):
    B, T = h.shape[0], h.shape[1]
    z = h @ w_in
    q, k, v, a, g, dq, dkv = jnp.split(z, SPLIT_POINTS, axis=-1)
    log_a = jax.nn.log_sigmoid((a @ w_gate_a2 + b_gate_a).astype(jnp.float32)) / GLA_TAU
    qh = q.reshape(B, T, GLA_HEADS, GLA_DK) * (GLA_DK ** -0.5)
    kh = k.reshape(B, T, GLA_HEADS, GLA_DK)
    vh = v.reshape(B, T, GLA_HEADS, GLA_DV)
    o, s_new = gla_recurrence(qh, kh, vh, log_a.reshape(B, T, GLA_HEADS, GLA_DK), gla_s0)
    o = rmsnorm(o.astype(h.dtype), gla_norm) * jax.nn.silu(g.reshape(B, T, GLA_HEADS, GLA_DV))
    o_gla = o.reshape(B, T, GLA_V_W)
    cq = rmsnorm(dq, q_norm)
    qm = (cq @ w_uq).reshape(B, T, MLA_HEADS, MLA_NOPE + MLA_ROPE)
    q_nope = qm[..., :MLA_NOPE]
    q_pe = apply_rope(qm[..., MLA_NOPE:], cos[:, None, :], sin[:, None, :])
    c_new = rmsnorm(dkv[..., :MLA_KV_LORA], kv_norm)
    kr_new = apply_rope(dkv[..., MLA_KV_LORA:], cos, sin)
    if past_c is None:
        o_mla = mla_self(q_nope, q_pe, c_new, kr_new, w_ukv)
    else:
        o_mla = mla_cached(q_nope, q_pe, c_new, kr_new, past_c, past_r, w_ukv)
    out = jnp.concatenate([o_gla, o_mla], axis=-1) @ w_out
    return out, s_new.astype(h.dtype), c_new, kr_new


def conv_module(h, buf, w_pw1, b_pw1, w_dw, b_dw, ln_g, ln_b, w_pw2, b_pw2):
    u = h @ w_pw1 + b_pw1
    u = u[..., :D_MODEL] * jax.nn.sigmoid(u[..., D_MODEL:])
    ext = jnp.concatenate([buf.astype(u.dtype), u], axis=1)
    y = lax.conv_general_dilated(ext, w_dw[:, None, :].astype(ext.dtype), window_strides=(1,),
                                 padding='VALID', dimension_numbers=('NWC', 'WIO', 'NWC'),
                                 feature_group_count=D_MODEL) + b_dw
    y = jax.nn.silu(layernorm(y, ln_g, ln_b))
    return y @ w_pw2 + b_pw2, ext[:, -(CONV_W - 1):]


def sq_relu_mlp(h, w_up, w_down):
    return jnp.square(jax.nn.relu(h @ w_up)) @ w_down


def setup_inputs(seed: int = 0) -> dict:
    key = jax.random.key(seed)
    ks = jax.random.split(key, 32)
    n_pages = PAST_LEN // PAGE_SIZE
    n_used = DEC_BATCH * n_pages
    n_pool = n_used + n_used // 4
    f32 = jnp.float32

    def nrm(k, shape, scale):
        return jax.random.normal(k, shape, f32) * scale

    def gain(k, shape):
        return 1.0 + 0.02 * jax.random.normal(k, shape, f32)

    page_table = jax.random.permutation(ks[6], n_pool)[:n_used].reshape(DEC_BATCH, n_pages).astype(jnp.int32)
    return {
        'x_prompt': nrm(ks[0], (BATCH, SEQ, D_MODEL), 1.0),
        'x_sample': nrm(ks[1], (DEC_BATCH, DEC_SEQ, D_MODEL), 1.0),
        'cache_kv': nrm(ks[2], (N_AB, n_pool, PAGE_SIZE, MLA_KV_LORA), 1.0),
        'cache_kr': nrm(ks[3], (N_AB, n_pool, PAGE_SIZE, MLA_ROPE), 1.0),
        'state_gla': nrm(ks[4], (N_AB, DEC_BATCH, GLA_HEADS, GLA_DK, GLA_DV), 0.5),
        'state_conv': nrm(ks[5], (N_C, DEC_BATCH, CONV_W - 1, D_MODEL), 0.5),
        'page_table': page_table,
        'norm_mix': gain(ks[7], (DEPTH, D_MODEL)),
        'norm_mlp': gain(ks[8], (DEPTH, D_MODEL)),
        'norm_final': gain(ks[9], (D_MODEL,)),
        'w_in': nrm(ks[10], (N_AB, D_MODEL, D_IN), D_MODEL ** -0.5),
        'w_gate_a2': nrm(ks[11], (N_AB, GLA_RANK, GLA_QK_W), GLA_RANK ** -0.5),
        'b_gate_a': nrm(ks[12], (N_AB, GLA_QK_W), 0.1),
        'gla_norm': gain(ks[13], (N_AB, GLA_DV)),
        'mla_q_norm': gain(ks[14], (N_AB, MLA_Q_LORA)),
        'mla_kv_norm': gain(ks[15], (N_AB, MLA_KV_LORA)),
        'w_uq': nrm(ks[16], (N_AB, MLA_Q_LORA, MLA_HEADS * (MLA_NOPE + MLA_ROPE)), MLA_Q_LORA ** -0.5),
        'w_ukv': nrm(ks[17], (N_AB, MLA_KV_LORA, MLA_HEADS * (MLA_NOPE + MLA_V)), MLA_KV_LORA ** -0.5),
        'w_out_ab': nrm(ks[18], (N_AB, D_MIX_OUT, D_MODEL), D_MIX_OUT ** -0.5),
        'w_pw1': nrm(ks[19], (N_C, D_MODEL, 2 * D_MODEL), D_MODEL ** -0.5),
        'b_pw1': nrm(ks[20], (N_C, 2 * D_MODEL), 0.02),
        'w_dw': nrm(ks[21], (N_C, CONV_W, D_MODEL), CONV_W ** -0.5),
        'b_dw': nrm(ks[22], (N_C, D_MODEL), 0.02),
        'conv_ln_g': gain(ks[23], (N_C, D_MODEL)),
        'conv_ln_b': nrm(ks[24], (N_C, D_MODEL), 0.02),
        'w_pw2': nrm(ks[25], (N_C, D_MODEL, D_MODEL), D_MODEL ** -0.5),
        'b_pw2': nrm(ks[26], (N_C, D_MODEL), 0.02),
        'w_up': nrm(ks[27], (DEPTH, D_MODEL, D_FF), D_MODEL ** -0.5),
        'w_down': nrm(ks[28], (DEPTH, D_FF, D_MODEL), 0.5 * D_FF ** -0.5),
    }


def reference(x_prompt, x_sample, cache_kv, cache_kr, state_gla, state_conv, page_table,
              norm_mix, norm_mlp, norm_final, w_in, w_gate_a2, b_gate_a, gla_norm,
              mla_q_norm, mla_kv_norm, w_uq, w_ukv, w_out_ab, w_pw1, b_pw1, w_dw, b_dw,
              conv_ln_g, conv_ln_b, w_pw2, b_pw2, w_up, w_down):
    Bp, S = x_prompt.shape[0], x_prompt.shape[1]
    Bd, T = x_sample.shape[0], x_sample.shape[1]
    past = page_table.shape[1] * PAGE_SIZE
    cos_p, sin_p = rope_table(jnp.arange(S))
    cos_s, sin_s = rope_table(past + jnp.arange(T))
    xp, xs = x_prompt, x_sample
    kv_p, kr_p, gla_p, conv_p = [], [], [], []
    kv_s, kr_s, gla_s, conv_s = [], [], [], []
    for l in range(DEPTH):
        i = l // 2
        hp = rmsnorm(xp, norm_mix[l])
        hs = rmsnorm(xs, norm_mix[l])
        if l % 2 == 0:
            ab_w = (w_in[i], w_gate_a2[i], b_gate_a[i], gla_norm[i], mla_q_norm[i],
                    mla_kv_norm[i], w_uq[i], w_ukv[i], w_out_ab[i])
            s0_p = jnp.zeros((Bp, GLA_HEADS, GLA_DK, GLA_DV), jnp.float32)
            mp, sp, cp, rp = ab_mixer(hp, s0_p, cos_p, sin_p, None, None, *ab_w)
            past_c = cache_kv[i][page_table].reshape(Bd, past, MLA_KV_LORA)
            past_r = cache_kr[i][page_table].reshape(Bd, past, MLA_ROPE)
            ms, ss, cs, rs = ab_mixer(hs, state_gla[i], cos_s, sin_s, past_c, past_r, *ab_w)
            kv_p.append(cp); kr_p.append(rp); gla_p.append(sp)
            kv_s.append(cs); kr_s.append(rs); gla_s.append(ss)
        else:
            c_w = (w_pw1[i], b_pw1[i], w_dw[i], b_dw[i], conv_ln_g[i], conv_ln_b[i], w_pw2[i], b_pw2[i])
            buf_p = jnp.zeros((Bp, CONV_W - 1, D_MODEL), xp.dtype)
            mp, bp = conv_module(hp, buf_p, *c_w)
            ms, bs = conv_module(hs, state_conv[i], *c_w)
            conv_p.append(bp); conv_s.append(bs)
        xp = xp + mp
        xs = xs + ms
        xp = xp + sq_relu_mlp(rmsnorm(xp, norm_mlp[l]), w_up[l], w_down[l])
        xs = xs + sq_relu_mlp(rmsnorm(xs, norm_mlp[l]), w_up[l], w_down[l])
    y_prompt = rmsnorm(xp, norm_final)
    y_sample = rmsnorm(xs, norm_final)
    return (y_prompt, y_sample,
            jnp.stack(kv_p), jnp.stack(kr_p), jnp.stack(gla_p), jnp.stack(conv_p),
            jnp.stack(kv_s), jnp.stack(kr_s), jnp.stack(gla_s), jnp.stack(conv_s))
```

```python
import numpy as np
from contextlib import ExitStack
import concourse.bass as bass
import concourse.mybir as mybir
from concourse.bass_utils import run_bass_kernel_spmd

F32 = mybir.dt.float32
BF16 = mybir.dt.bfloat16
I32 = mybir.dt.int32
AF = mybir.ActivationFunctionType
ALU = mybir.AluOpType
SP_ENG = mybir.EngineType.SP

NCORES = 8
D = 1024
KC = 8
SEQ = 2048
NS = 64
NSEQ = 16
NT = SEQ + NS
DEPTH = 4
D_IN = 2096
DFF = 4096
EPS = 1e-6
PAST = 8192
NPAGES = 64
NPOOL = 10240
MLA_SCALE = 96 ** -0.5
TILES = [(0, 512), (512, 512), (1024, 512), (1536, 512), (2048, 64)]
CONV_NW = 512

STAGE = 99
SAME_ENGINE_SYNC = True
DBG = set()


class Sem:
    __slots__ = ("h", "val", "name", "is_dma")

    def __init__(self, h, name, is_dma=False):
        self.h = h
        self.val = 0
        self.name = name
        self.is_dma = is_dma


class Buf:
    __slots__ = ("name", "w", "wx", "r", "pend")

    def __init__(self, name):
        self.name = name
        self.wx = []
        self.w = None
        self.r = {}
        self.pend = None


class Eng:
    def __init__(self, name, h, sem):
        self.name = name
        self.h = h
        self.sem = sem
        self.seen = {}
        self.pend_r = []
        self.pend_w = []


class FW:
    def __init__(self, nc, es, n_dma_sems=40):
        self.nc = nc
        self.es = es
        mk = lambda n: Sem(es.enter_context(nc.semaphore(n)), n, is_dma=n.startswith(("s_dma", "s_gth")))
        self.pe = Eng("pe", nc.tensor, mk("s_pe"))
        self.act = Eng("act", nc.scalar, mk("s_act"))
        self.dve = Eng("dve", nc.vector, mk("s_dve"))
        self.pool = Eng("pool", nc.gpsimd, mk("s_pool"))
        self.sp = Eng("sp", nc.sync, mk("s_sp"))
        self.engs = [self.pe, self.act, self.dve, self.pool, self.sp]
        self.dsems = [mk(f"s_dma{i}") for i in range(n_dma_sems)]
        self.di = 0
        self.gsems = [mk(f"s_gth{i}") for i in range(24)]
        self.gi = 0
        self.n_wait = 0
        self.n_op = 0

    def _wait(self, eng, tok, pe_ok=False):
        if tok is None:
            return
        s, v = tok
        if s is eng.sem and (not SAME_ENGINE_SYNC or (pe_ok and eng is self.pe)):
            return
        if eng.seen.get(s, 0) >= v:
            return
        eng.h.wait_ge(s.h, v)
        eng.seen[s] = v
        self.n_wait += 1

    def _deps(self, eng, reads, writes, dma_group=False):
        for b in reads:
            assert b.pend is None or b.pend is eng, f"{b.name} has pending access by {b.pend.name}"
            self._wait(eng, b.w)
            for tok in b.wx:
                self._wait(eng, tok)
        for b in writes:
            assert b.pend is None or b.pend is eng, f"{b.name} has pending access by {b.pend.name}"
            if dma_group and b.w is not None and b.w[0].is_dma and not b.r:
                continue
            self._wait(eng, b.w, pe_ok=True)
            for tok in b.wx:
                self._wait(eng, tok)
            for s, v in list(b.r.items()):
                self._wait(eng, (s, v))

    def op(self, eng, fn, reads=(), writes=(), inc=True):
        self._deps(eng, reads, writes)
        inst = fn()
        self.n_op += 1
        if not inc:
            for b in reads:
                b.pend = eng
                eng.pend_r.append(b)
            for b in writes:
                b.pend = eng
                eng.pend_w.append(b)
            return inst
        eng.sem.val += 1
        inst.then_inc(eng.sem.h, 1)
        tok = (eng.sem, eng.sem.val)
        for b in list(writes) + eng.pend_w:
            b.w = tok
            b.wx = []
            b.r = {}
            b.pend = None
        for b in list(reads) + eng.pend_r:
            if b.w is not tok:
                b.r[eng.sem] = eng.sem.val
            b.pend = None
        eng.pend_r = []
        eng.pend_w = []
        return inst

    def dma(self, q, out, in_, reads=(), writes=(), **kw):
        assert not q.pend_r and not q.pend_w
        self._deps(q, reads, writes, dma_group=True)
        s = self.dsems[self.di]
        self.di = (self.di + 1) % len(self.dsems)
        if s.val:
            self._wait(q, (s, s.val))
        inst = q.h.dma_start(out=out, in_=in_, **kw)
        s.val += 16
        inst.then_inc(s.h, 16)
        tok = (s, s.val)
        self._dma_written(writes, tok)
        for b in reads:
            b.r[s] = s.val
        return tok

    def _dma_written(self, writes, tok):
        for b in writes:
            if b.w is not None and b.w[0].is_dma and not b.r:
                b.wx = [t for t in b.wx if t[0] is not tok[0]] + ([b.w] if b.w[0] is not tok[0] else [])
            else:
                b.wx = []
            b.w = tok
            b.r = {}

    def gather(self, out, in_rows, idx_ap, reads=(), writes=()):
        q = self.pool
        assert not q.pend_r and not q.pend_w
        self._deps(q, reads, writes, dma_group=True)
        s = self.gsems[self.gi]
        self.gi = (self.gi + 1) % len(self.gsems)
        if s.val:
            self._wait(q, (s, s.val))
        inst = self.nc.gpsimd.indirect_dma_start(out=out, out_offset=None, in_=in_rows,
                                                 in_offset=bass.IndirectOffsetOnAxis(ap=idx_ap, axis=0))
        s.val += 16
        inst.then_inc(s.h, 16)
        tok = (s, s.val)
        self._dma_written(writes, tok)
        for b in reads:
            b.r[s] = s.val
        return tok

    def wait_buf(self, eng, b):
        self._wait(eng, b.w)

    def barrier(self):
        for e in self.engs:
            assert not e.pend_r and not e.pend_w, e.name
        for e in self.engs:
            for s in self.dsems + self.gsems:
                if s.val:
                    self._wait(e, (s, s.val))
            for f in self.engs:
                if f.sem.val:
                    self._wait(e, (f.sem, f.sem.val))

    def finish(self):
        self.barrier()


_uid = [0]


def _sbuf(nc, name, shape, dtype):
    _uid[0] += 1
    return nc.sbuf_tensor(f"{name}_u{_uid[0]}", list(shape), dtype)


def _psum(nc, name, shape, dtype):
    _uid[0] += 1
    return nc.psum_tensor(f"{name}_u{_uid[0]}", list(shape), dtype)


class Ring:
    def __init__(self, nc, es, name, n, shape, dtype, psum=False):
        self.tiles = []
        self.bufs = []
        for i in range(n):
            if psum:
                t = es.enter_context(_psum(nc, f"{name}{i}", shape, dtype))
            else:
                t = es.enter_context(_sbuf(nc, f"{name}{i}", shape, dtype))
            self.tiles.append(t)
            self.bufs.append(Buf(f"{name}{i}"))
        self.i = 0

    def next(self):
        t, b = self.tiles[self.i], self.bufs[self.i]
        self.i = (self.i + 1) % len(self.tiles)
        return t, b


class Builder:
    def __init__(self):
        self.nc = bass.Bass("TRN2", target_bir_lowering=False)
        self.dram = {}

    def din(self, name, shape, dtype=F32):
        self.dram[name] = self.nc.dram_tensor(name, list(shape), dtype, kind="ExternalInput").ap()
        return self.dram[name]

    def dout(self, name, shape, dtype=F32):
        self.dram[name] = self.nc.dram_tensor(name, list(shape), dtype, kind="ExternalOutput").ap()
        return self.dram[name]

    def build(self):
        nc = self.nc
        d = self.dram
        self.din("xin", [D, NT])
        self.din("rope_cos", [32, NT])
        self.din("rope_sin", [32, NT])
        self.din("mask4", [64, 64])
        self.din("maskq", [64, NSEQ * 32])
        self.din("selb", [64, NSEQ])
        self.din("page_table", [128, NSEQ * NPAGES], I32)
        self.din("iota_p", [128, 1])
        if "nocache" not in DBG:
            self.din("cache_kv", [2, NPOOL, 128, 256])
            self.din("cache_kr", [2, NPOOL, 128, 32])
        self.din("state_gla", [2, NSEQ, 4, 64, 128])
        self.din("state_conv", [2, D, NSEQ, 30])
        self.din("norm_mix", [DEPTH, D])
        self.din("norm_mlp", [DEPTH, D])
        self.din("norm_final", [1, D])
        self.din("w_in", [2, D, D_IN])
        self.din("w_gate_a2", [2, 16, 256])
        self.din("b_gate_a", [2, 256])
        self.din("gla_norm", [2, 128])
        self.din("mla_q_norm", [2, 256])
        self.din("mla_kv_norm", [2, 256])
        self.din("w_uq", [2, 256, 768])
        self.din("w_ukv", [2, 256, 1024])
        self.din("w_ukT", [2, 64, 8, 256])
        self.din("w_out_ab", [2, D, D])
        self.din("w_pw1", [2, D, 2 * D])
        self.din("b_pw1", [2, 2 * D])
        self.din("w_dwT", [2, D, 31])
        self.din("b_dw", [2, D])
        self.din("conv_ln_g", [2, D])
        self.din("conv_ln_b", [2, D])
        self.din("w_pw2", [2, D, D])
        self.din("b_pw2", [2, D])
        self.din("w_up", [DEPTH, D, DFF])
        self.din("w_down", [DEPTH, DFF, D])
        self.dout("y", [D, NT])
        self.dout("kv", [2, 256, NT])
        self.dout("kr", [2, 32, NT])
        self.dout("gla_p", [2, 4, 64, 128])
        self.dout("gla_s", [2, NSEQ, 4, 64, 128])
        self.dout("conv_p", [2, D, 30])
        self.dout("conv_s", [2, D, NSEQ, 30])

        with ExitStack() as es:
            self.es = es
            self.fw = fw = FW(nc, es)
            sb = lambda n, s, dt: es.enter_context(_sbuf(nc, n, s, dt))
            self.x = sb("x", [128, KC, NT], F32)
            self.xb = [Buf(f"x{t}") for t in range(5)]
            self.ones_bf = sb("ones_bf", [128, 128], BF16)
            self.ident_f = sb("ident_f", [128, 128], F32)
            self.ident_b = sb("ident_b", [128, 128], BF16)
            self.tri_b = sb("tri_b", [128, 128], BF16)
            self.tri2_f = sb("tri2_f", [128, 128], F32)
            self.tri2_b = sb("tri2_b", [128, 128], BF16)
            self.m4_f = sb("m4_f", [64, 64], F32)
            self.m4_b = sb("m4_b", [64, 64], BF16)
            self.maskq = sb("maskq", [64, NSEQ * 32], F32)
            self.selb = sb("selb", [64, NSEQ], F32)
            self.ropeT = sb("ropeT", [96, NT], F32)
            self.gmix = sb("gmix", [128, DEPTH, KC], F32)
            self.gmlp = sb("gmlp", [128, DEPTH, KC], F32)
            self.gfin = sb("gfin", [128, KC], F32)
            self.ptab = sb("ptab", [128, NSEQ * NPAGES], I32)
            self.iota = sb("iota_p", [128, 1], F32)
            self.epsT = sb("epsT", [128, 1], F32)
            self.oneT = sb("oneT", [128, 1], F32)
            self.cb = Buf("consts")
            self.ptb = Buf("ptab")
            self.load_consts()
            self.load_x()
            for l in range(DEPTH):
                if STAGE < 0:
                    break
                if l % 2 == 0:
                    self.ab_layer(l // 2, l)
                else:
                    self.conv_layer(l // 2, l)
                if STAGE <= 2 * l + 1:
                    break
                self.mlp(l)
                if STAGE <= 2 * l + 2:
                    break
            self.final_norm()
            fw.finish()
        return nc

    def load_consts(self):
        nc, fw, d = self.nc, self.fw, self.dram
        cb = self.cb
        fw.op(fw.dve, lambda: nc.vector.memset(self.ones_bf[:], 1.0), writes=[cb])
        fw.op(fw.dve, lambda: nc.vector.memset(self.epsT[:], EPS), writes=[cb])
        fw.op(fw.dve, lambda: nc.vector.memset(self.oneT[:], 1.0), writes=[cb])
        fw.op(fw.pool, lambda: nc.gpsimd.memset(self.ident_f[:], 1.0), writes=[cb])
        fw.op(fw.pool, lambda: nc.gpsimd.affine_select(
            out=self.ident_f[:], in_=self.ident_f[:], pattern=[[-1, 128]], compare_op=ALU.is_equal,
            fill=0.0, base=0, channel_multiplier=1), writes=[cb])
        fw.op(fw.pool, lambda: nc.gpsimd.tensor_copy(out=self.ident_b[:], in_=self.ident_f[:]), writes=[cb])
        fw.op(fw.pool, lambda: nc.gpsimd.memset(self.tri2_f[:], 1.0), writes=[cb])
        fw.op(fw.pool, lambda: nc.gpsimd.affine_select(
            out=self.tri2_f[:], in_=self.tri2_f[:], pattern=[[1, 128]], compare_op=ALU.is_ge,
            fill=0.0, base=0, channel_multiplier=-1), writes=[cb])
        fw.op(fw.pool, lambda: nc.gpsimd.tensor_copy(out=self.tri_b[:], in_=self.tri2_f[:]), writes=[cb])
        fw.op(fw.pool, lambda: nc.gpsimd.affine_select(
            out=self.tri2_f[0:64, :], in_=self.tri2_f[0:64, :], pattern=[[-1, 128]], compare_op=ALU.is_ge,
            fill=0.0, base=63, channel_multiplier=0), writes=[cb])
        fw.op(fw.pool, lambda: nc.gpsimd.tensor_copy(out=self.tri2_b[:], in_=self.tri2_f[:]), writes=[cb])
        fw.dma(fw.sp, self.m4_f[:], d["mask4"], writes=[cb])
        fw.dma(fw.sp, self.maskq[:], d["maskq"], writes=[cb])
        fw.dma(fw.sp, self.selb[:], d["selb"], writes=[cb])
        fw.dma(fw.sp, self.ropeT[64:96, :], d["rope_cos"], writes=[cb])
        fw.dma(fw.sp, self.ropeT[32:64, :], d["rope_sin"], writes=[cb])
        fw.dma(fw.sp, self.gmix[:], d["norm_mix"].rearrange("l (kc p) -> p l kc", p=128), writes=[cb],
               allow_slow_non_contiguous=True)
        fw.dma(fw.sp, self.gmlp[:], d["norm_mlp"].rearrange("l (kc p) -> p l kc", p=128), writes=[cb],
               allow_slow_non_contiguous=True)
        fw.dma(fw.sp, self.gfin[:], d["norm_final"].rearrange("o (kc p) -> p (o kc)", p=128), writes=[cb],
               allow_slow_non_contiguous=True)
        fw.dma(fw.sp, self.ptab[:], d["page_table"], writes=[self.ptb])
        fw.dma(fw.sp, self.iota[:], d["iota_p"], writes=[cb])
        fw.op(fw.dve, lambda: nc.vector.tensor_scalar(out=self.ptab[:], in0=self.ptab[:], scalar1=128.0,
                                                      scalar2=self.iota[:, 0:1], op0=ALU.mult, op1=ALU.add),
              reads=[self.ptb, cb], writes=[self.ptb])
        fw.op(fw.pool, lambda: nc.gpsimd.tensor_copy(out=self.m4_b[:], in_=self.m4_f[:]), reads=[cb], writes=[cb])

    def load_x(self):
        fw, d = self.fw, self.dram
        xin = d["xin"].rearrange("(kc p) t -> p kc t", p=128)
        for t, (t0, n) in enumerate(TILES):
            for kh in range(2):
                fw.dma(fw.sp, self.x[:, 4 * kh:4 * kh + 4, t0:t0 + n], xin[:, 4 * kh:4 * kh + 4, t0:t0 + n],
                       writes=[self.xb[t]])

    def rstd_from_ssq(self, ps_ap, n_feat, out_ap, tmp_ap, reads, rb, tb, p0=0):
        nc, fw = self.nc, self.fw
        npart = ps_ap.shape[0]
        fw.op(fw.act, lambda: nc.scalar.activation(out=tmp_ap, in_=ps_ap, func=AF.Sqrt,
                                                   scale=1.0 / n_feat, bias=self.epsT[p0:p0 + npart, 0:1]),
              reads=list(reads) + [self.cb], writes=[tb])
        fw.op(fw.dve, lambda: nc.vector.reciprocal(out=out_ap, in_=tmp_ap), reads=[tb], writes=[rb])

    def norm_tile(self, R, gains_ap, t, t0, n, out_ap, outb):
        nc, fw = self.nc, self.fw
        sqt, sqb = R["sq"].next()
        pst, psb = R["ps"].next()
        tt, tb = R["tmp"].next()
        rt, rb = R["rs"].next()
        fw.op(fw.act, lambda: nc.scalar.activation(out=sqt[:, :, :n], in_=self.x[:, :, t0:t0 + n], func=AF.Square),
              reads=[self.xb[t]], writes=[sqb])
        for kc in range(KC):
            fw.op(fw.pe, lambda: nc.tensor.matmul(pst[:, :n], lhsT=self.ones_bf[:], rhs=sqt[:, kc, :n],
                                                  start=(kc == 0), stop=(kc == KC - 1)),
                  reads=[sqb, self.cb], writes=[psb], inc=(kc == KC - 1))
        self.rstd_from_ssq(pst[:, :n], D, rt[:, :n], tt[:, :n], [psb], rb, tb)
        for kc in range(KC):
            fw.op(fw.dve, lambda: nc.vector.scalar_tensor_tensor(
                out=out_ap[:, kc, :n], in0=self.x[:, kc, t0:t0 + n], scalar=gains_ap[:, kc:kc + 1],
                in1=rt[:, :n], op0=ALU.mult, op1=ALU.mult),
                  reads=[self.xb[t], rb, self.cb], writes=[outb])

    def norm_rings(self, es, n, nps=2):
        nc = self.nc
        return {"sq": Ring(nc, es, "nsq", 2, [128, KC, n], BF16),
                "ps": Ring(nc, es, "nps", nps, [128, 512], F32, psum=True),
                "tmp": Ring(nc, es, "ntmp", 2, [128, n], F32),
                "rs": Ring(nc, es, "nrs", 2, [128, n], F32)}

    def final_norm(self):
        nc, fw, d = self.nc, self.fw, self.dram
        yv = d["y"].rearrange("(kc p) t -> p kc t", p=128)
        with ExitStack() as es:
            R = self.norm_rings(es, 512)
            yo = Ring(nc, es, "fyo", 2, [128, KC, 512], F32)
            for t, (t0, n) in enumerate(TILES):
                if "rawx" in DBG:
                    fw.dma(fw.sp, yv[:, :, t0:t0 + n], self.x[:, :, t0:t0 + n], reads=[self.xb[t]])
                    continue
                yt, yb = yo.next()
                self.norm_tile(R, self.gfin, t, t0, n, yt, yb)
                fw.dma(fw.sp, yv[:, :, t0:t0 + n], yt[:, :, :n], reads=[yb])
            fw.barrier()

    def load_cast(self, es_stage, dst_bf, dstb, src_ap, shape, name):
        nc, fw = self.nc, self.fw
        st = es_stage.enter_context(_sbuf(nc, name, shape, F32))
        stb = Buf(name)
        fw.dma(fw.sp, st[:], src_ap, writes=[stb])
        fw.op(fw.pool, lambda: nc.gpsimd.tensor_copy(out=dst_bf, in_=st[:]), reads=[stb], writes=[dstb])
        return st, stb

    def mm(self, out, lhsT, rhs, start, stop, reads, writes, inc=None):
        nc, fw = self.nc, self.fw
        if inc is None:
            inc = stop
        return fw.op(fw.pe, lambda: nc.tensor.matmul(out, lhsT=lhsT, rhs=rhs, start=start, stop=stop),
                     reads=reads, writes=writes, inc=inc)

    def load_w_cols(self, es, src2d, c0, cw, name, extra=None):
        nc, fw = self.nc, self.fw
        w = es.enter_context(_sbuf(nc, name, [128, KC, cw], BF16))
        wb = Buf(name)
        wv = src2d.rearrange("(kc p) n -> p kc n", p=128)
        with ExitStack() as es2:
            stg = Ring(nc, es2, "wstg", 2, [128, KC, 512], F32)
            for b0 in range(0, cw, 512):
                bw = min(512, cw - b0)
                st, stb = stg.next()
                for kh in range(2):
                    fw.dma(fw.sp, st[:, 4 * kh:4 * kh + 4, :bw], wv[:, 4 * kh:4 * kh + 4, c0 + b0:c0 + b0 + bw],
                           writes=[stb])
                fw.op(fw.pool, lambda: nc.gpsimd.tensor_copy(out=w[:, :, b0:b0 + bw], in_=st[:, :, :bw]),
                      reads=[stb], writes=[wb])
                if extra is not None:
                    extra(st, stb, b0, bw)
            fw.barrier()
        return w, wb

    def ab_layer(self, i, l):
        nc, fw, d = self.nc, self.fw, self.dram
        with ExitStack() as es:
            sb = lambda n, s, dt: es.enter_context(_sbuf(nc, n, s, dt))
            L = type("L", (), {})()
            L.i = i
            L.prm = sb("abprm", [128, 8], F32)
            L.prmb = Buf("abprm")
            fw.dma(fw.sp, L.prm[:, 0:1], d["gla_norm"][i:i + 1, :].rearrange("o p -> p o"), writes=[L.prmb],
                   allow_slow_non_contiguous=True)
            fw.dma(fw.sp, L.prm[:, 1:3], d["mla_q_norm"][i:i + 1, :].rearrange("o (g p) -> p (o g)", p=128),
                   writes=[L.prmb], allow_slow_non_contiguous=True)
            fw.dma(fw.sp, L.prm[:, 3:5], d["mla_kv_norm"][i:i + 1, :].rearrange("o (g p) -> p (o g)", p=128),
                   writes=[L.prmb], allow_slow_non_contiguous=True)
            L.og = sb("og", [128, 4, NT], BF16)
            L.ogb = Buf("og")
            if "nogla" not in DBG:
                self.gla_pass(L, l)
            fw.barrier()
            L.cqT = sb("cqT", [128, 2, NT], BF16)
            L.ckvT = sb("ckvT", [128, 2, NT], BF16)
            L.cqb = Buf("cqT")
            L.ckvb = Buf("ckvT")
            L.K = [sb(f"Kt{j}", [96, NT], BF16) for j in range(2)]
            L.Kb = [Buf(f"Kt{j}") for j in range(2)]
            if "nomla" not in DBG:
                self.mla_pass(L, l)
            fw.barrier()
            if STAGE <= 2 * l:
                return
            L.om = sb("om", [128, 4, NT], BF16)
            L.omb = Buf("om")
            self.ab_phase_b(L)
            fw.barrier()
            self.ab_phase_c(L)
            fw.barrier()

    def mla_pass(self, L, l):
        nc, fw, d = self.nc, self.fw, self.dram
        i = L.i
        with ExitStack() as es:
            sb = lambda n, s, dt: es.enter_context(_sbuf(nc, n, s, dt))
            w_sw = sb("w_sw_b", [128, KC, 32], BF16)
            wswb = Buf("w_sw")

            def extra(st, stb, b0, bw):
                if b0 == 512:
                    fw.op(fw.pool, lambda: nc.gpsimd.tensor_copy(out=w_sw[:, :, 0:16], in_=st[:, :, 16:32]),
                          reads=[stb], writes=[wswb])
                    fw.op(fw.pool, lambda: nc.gpsimd.tensor_copy(out=w_sw[:, :, 16:32], in_=st[:, :, 0:16]),
                          reads=[stb], writes=[wswb])
            w_m, wb = self.load_w_cols(es, d["w_in"][i], 1552, 544, "w_m", extra)
            R = self.norm_rings(es, 512, nps=1)
            hr = Ring(nc, es, "hr", 2, [128, KC, 512], BF16)
            pj = Ring(nc, es, "pj", 2, [128, 2, 512], F32, psum=True)
            pq = Ring(nc, es, "pq", 2, [128, 512], F32, psum=True)
            ckvf = Ring(nc, es, "ckvf", 2, [128, 2, 512], F32)
            sq2 = Ring(nc, es, "sq2", 2, [128, 2, 512], BF16)
            rs2 = Ring(nc, es, "rs2", 2, [128, 512], F32)
            tm2 = Ring(nc, es, "tm2", 2, [128, 512], F32)
            kp1 = sb("kp1", [96, 512], F32); kp1b = Buf("kp1")
            kp2 = sb("kp2", [96, 512], F32); kp2b = Buf("kp2")
            kpo = Ring(nc, es, "kpo", 2, [96, 512], F32)
            gains = self.gmix[:, l, :]
            kv_out = d["kv"][i].rearrange("(g p) t -> p g t", p=128)
            kr_out = d["kr"][i]
            for t, (t0, n) in enumerate(TILES):
                ht, hb = hr.next()
                self.norm_tile(R, gains, t, t0, n, ht, hb)

                def proj(ps_ap, wtile, c0, cw, psb, wbuf):
                    for kc in range(KC):
                        self.mm(ps_ap, wtile[:, kc, c0:c0 + cw], ht[:, kc, :n], kc == 0, kc == KC - 1,
                                [wbuf, hb], [psb])
                for which in range(2):
                    c0 = 0 if which == 0 else 256
                    pcol = 1 if which == 0 else 3
                    pa, pab = pj.next()
                    for g in range(2):
                        proj(pa[:, g, :n], w_m, c0 + g * 128, 128, pab, wb)
                    s2, s2b = sq2.next()
                    r2, r2b = rs2.next()
                    t2, t2b = tm2.next()
                    fw.op(fw.act, lambda: nc.scalar.activation(out=s2[:, :, :n], in_=pa[:, :, :n], func=AF.Square),
                          reads=[pab], writes=[s2b])
                    pqt, pqb = pq.next()
                    for g in range(2):
                        self.mm(pqt[:, :n], self.ones_bf[:], s2[:, g, :n], g == 0, g == 1, [s2b, self.cb], [pqb])
                    self.rstd_from_ssq(pqt[:, :n], 256, r2[:, :n], t2[:, :n], [pqb], r2b, t2b)
                    if which == 0:
                        for g in range(2):
                            fw.op(fw.dve, lambda: nc.vector.scalar_tensor_tensor(
                                out=L.cqT[:, g, t0:t0 + n], in0=pa[:, g, :n],
                                scalar=L.prm[:, pcol + g:pcol + g + 1], in1=r2[:, :n], op0=ALU.mult, op1=ALU.mult),
                                  reads=[pab, r2b, L.prmb], writes=[L.cqb])
                    else:
                        cf, cfb = ckvf.next()
                        for g in range(2):
                            fw.op(fw.dve, lambda: nc.vector.scalar_tensor_tensor(
                                out=cf[:, g, :n], in0=pa[:, g, :n],
                                scalar=L.prm[:, pcol + g:pcol + g + 1], in1=r2[:, :n], op0=ALU.mult, op1=ALU.mult),
                                  reads=[pab, r2b, L.prmb], writes=[cfb])
                        fw.op(fw.pool, lambda: nc.gpsimd.tensor_copy(out=L.ckvT[:, :, t0:t0 + n], in_=cf[:, :, :n]),
                              reads=[cfb], writes=[L.ckvb])
                        fw.dma(fw.sp, kv_out[:, :, t0:t0 + n], cf[:, :, :n], reads=[cfb])
                pa, pab = pj.next()
                proj(pa[64:96, 0, :n], w_m, 512, 32, pab, wb)
                proj(pa[64:96, 1, :n], w_sw, 0, 32, pab, wswb)
                ko, kob = kpo.next()
                fw.op(fw.dve, lambda: nc.vector.tensor_tensor(out=kp1[64:96, :n], in0=pa[64:96, 0, :n],
                                                              in1=self.ropeT[64:96, t0:t0 + n], op=ALU.mult),
                      reads=[pab, self.cb], writes=[kp1b])
                fw.op(fw.dve, lambda: nc.vector.tensor_tensor(out=kp2[64:96, :n], in0=pa[64:96, 1, :n],
                                                              in1=self.ropeT[32:64, t0:t0 + n], op=ALU.mult),
                      reads=[pab, self.cb], writes=[kp2b])
                fw.op(fw.pool, lambda: nc.gpsimd.tensor_tensor(out=ko[64:96, :n], in0=kp1[64:96, :n],
                                                               in1=kp2[64:96, :n], op=ALU.add),
                      reads=[kp1b, kp2b], writes=[kob])
                for j in range(2):
                    fw.op(fw.pool, lambda: nc.gpsimd.tensor_copy(out=L.K[j][64:96, t0:t0 + n], in_=ko[64:96, :n]),
                          reads=[kob], writes=[L.Kb[j]])
                fw.dma(fw.sp, kr_out[:, t0:t0 + n], ko[64:96, :n], reads=[kob])
            fw.barrier()

    def gla_pass(self, L, l):
        nc, fw, d = self.nc, self.fw, self.dram
        i = L.i
        with ExitStack() as es:
            sb = lambda n, s, dt: es.enter_context(_sbuf(nc, n, s, dt))
            w_g, wb = self.load_w_cols(es, d["w_in"][i], 0, 1552, "w_g")
            w2 = sb("w2aug", [32, 256], F32)
            w2b = Buf("w2aug")
            fw.dma(fw.sp, w2[0:16, :], d["w_gate_a2"][i], writes=[w2b])
            fw.dma(fw.sp, w2[16:17, :], d["b_gate_a"][i:i + 1, :], writes=[w2b])
            S = sb("S", [128, 2, 128], F32); Sb = Buf("S")
            fw.op(fw.dve, lambda: nc.vector.memset(S[:], 0.0), writes=[Sb])
            self.gla_tiles(L, l, w_g, wb, w2, w2b, S, Sb, [(256 * j, 256) for j in range(8)], False)
            fw.barrier()
            if "nosample" not in DBG:
                self.gla_tiles(L, l, w_g, wb, w2, w2b, S, Sb, [(SEQ, NS)], True)
            fw.barrier()

    def gla_tiles(self, L, l, w_g, wb, w2, w2b, S, Sb, tiles, sample):
        nc, fw, d = self.nc, self.fw, self.dram
        i = L.i
        NW = NS if sample else 256
        with ExitStack() as es:
            sb = lambda n, s, dt: es.enter_context(_sbuf(nc, n, s, dt))
            R = self.norm_rings(es, NW, nps=1)
            hr = Ring(nc, es, "hr", 2, [128, KC, NW], BF16)
            pj = Ring(nc, es, "pj", 2, [128, 512], F32, psum=True)
            g0 = es.enter_context(_psum(nc, "g0", [128, 512], F32)); g0b = Buf("g0")
            g1 = es.enter_context(_psum(nc, "g1", [128, 2, 256], F32)); g1b = Buf("g1")
            g2 = es.enter_context(_psum(nc, "g2", [128, 4, 128], F32)); g2b = Buf("g2")
            g3 = es.enter_context(_psum(nc, "g3", [128, 4, 128], F32)); g3b = Buf("g3")
            g4 = es.enter_context(_psum(nc, "g4", [128, 2, 2, 128], F32)); g4b = Buf("g4")
            qf = sb("qf", [128, 2, NW], F32); qfb = Buf("qf")
            kf = sb("kf", [128, 2, NW], F32); kfb = Buf("kf")
            sg = sb("sg", [128, 4, NW], F32); sgb = Buf("sg")
            aT = sb("aT", [32, NW], F32); aTb = Buf("aT")
            vtok_R = Ring(nc, es, "vtok", 2, [128, 512], BF16)
            ex_R = Ring(nc, es, "ex", 2, [128, 256], F32)
            nl_R = Ring(nc, es, "nl", 2, [128, 256], F32)
            ektok_R = Ring(nc, es, "ektok", 2, [128, 256], F32)
            kttok_R = Ring(nc, es, "kttok", 2, [128, 256], BF16)
            eqT_R = Ring(nc, es, "eqT", 2, [128, 2, 128], F32)
            ekT_R = Ring(nc, es, "ekT", 2, [128, 2, 128], F32)
            qbd_R = Ring(nc, es, "qbd", 2, [128, 2, 2, 128], BF16)
            kbd_R = Ring(nc, es, "kbd", 2, [128, 2, 256], BF16)
            ktT_R = Ring(nc, es, "ktT", 2, [128, 2, 128], BF16)
            am_R = Ring(nc, es, "am", 2, [128, 4, 128], BF16)
            Sring = Ring(nc, es, "Sbf", 4, [128, 2, 128], BF16)
            Sbf, Sbfb = Sring.next()
            stmp_R = Ring(nc, es, "stmp", 2, [128, 2, 2, 128], F32)
            osq_R = Ring(nc, es, "osq", 2, [128, 4, 128], BF16)
            ors_R = Ring(nc, es, "ors", 2, [128, 4, 128], F32)
            otm_R = Ring(nc, es, "otm", 2, [128, 4, 128], F32)
            fw.op(fw.dve, lambda: nc.vector.memset(aT[:], 1.0), writes=[aTb])
            fw.op(fw.dve, lambda: nc.vector.memset(Sbf[:], 0.0), writes=[Sbfb])
            for t_, b_ in zip(qbd_R.tiles + kbd_R.tiles, qbd_R.bufs + kbd_R.bufs):
                fw.op(fw.dve, lambda: nc.vector.memset(t_[:], 0.0), writes=[b_])
            if sample:
                stS = sb("stS", [128, NSEQ, 2, 128], F32); stSb = Buf("stS")
                sv = d["state_gla"][i].rearrange("b (g two) dd e -> two dd b g e", two=2)
                for two in range(2):
                    fw.dma(fw.sp, stS[two * 64:(two + 1) * 64, :, :, :], sv[two], writes=[stSb])
                stSbf = Ring(nc, es, "stSbf", 2, [128, 2, 2, 128], BF16)
                kmask = Ring(nc, es, "kmask", 2, [64, 256], BF16)
                oin = sb("oin", [128, 4, 64], F32); oinb = Buf("oin")
                of = sb("of", [128, 4, 64], F32); ofb = Buf("of")
            gains = self.gmix[:, l, :]
            for (t0, n) in tiles:
                t = min(t0 // 512, 4)
                ht, hb = hr.next()
                self.norm_tile(R, gains, t, t0, n, ht, hb)

                def proj(ps_ap, c0, cw, psb):
                    for kc in range(KC):
                        self.mm(ps_ap, w_g[:, kc, c0:c0 + cw], ht[:, kc, :n], kc == 0, kc == KC - 1, [wb, hb], [psb])
                for g in range(2):
                    pst, psb = pj.next()
                    proj(pst[:, :n], g * 128, 128, psb)
                    fw.op(fw.act, lambda: nc.scalar.copy(out=qf[:, g, :n], in_=pst[:, :n]), reads=[psb], writes=[qfb])
                for g in range(2):
                    pst, psb = pj.next()
                    proj(pst[:, :n], 256 + g * 128, 128, psb)
                    fw.op(fw.act, lambda: nc.scalar.copy(out=kf[:, g, :n], in_=pst[:, :n]), reads=[psb], writes=[kfb])
                for m in range(4):
                    pst, psb = pj.next()
                    proj(pst[:, :n], 1040 + m * 128, 128, psb)
                    fw.op(fw.act, lambda: nc.scalar.activation(out=sg[:, m, :n], in_=pst[:, :n], func=AF.Silu),
                          reads=[psb], writes=[sgb])
                pst, psb = pj.next()
                proj(pst[0:16, :n], 1024, 16, psb)
                fw.op(fw.act, lambda: nc.scalar.copy(out=aT[0:16, :n], in_=pst[0:16, :n]), reads=[psb], writes=[aTb])

                if "cut1" in DBG:
                    continue
                nsub = 1 if sample else n // 128
                for si in range(nsub):
                    s0 = si * 128
                    nt_ = NS if sample else 128
                    tri_f = self.m4_f if sample else self.tri2_f
                    vtok, vtb = vtok_R.next()
                    ex, exb = ex_R.next()
                    nl, nlb = nl_R.next()
                    ektok, ektb = ektok_R.next()
                    kttok, kttb = kttok_R.next()
                    eqT, eqb = eqT_R.next()
                    ekT, ekb = ekT_R.next()
                    ktT, ktb = ktT_R.next()
                    am, amb = am_R.next()
                    stmp, stmpb = stmp_R.next()
                    osq, osqb = osq_R.next()
                    ors, orsb = ors_R.next()
                    otm, otmb = otm_R.next()
                    qbd, qtb = qbd_R.next()
                    kbd, kbdb = kbd_R.next()
                    pk, pkb = pj.next()
                    for kc in range(KC):
                        self.mm(pk[0:nt_, 0:256], ht[:, kc, s0:s0 + nt_], w_g[:, kc, 256:512], kc == 0, kc == KC - 1,
                                [wb, hb], [pkb])
                    self.mm(g0[0:nt_, 0:256], aT[0:17, s0:s0 + nt_], w2[0:17, :], True, True, [aTb, w2b], [g0b])
                    fw.op(fw.act, lambda: nc.scalar.activation(out=ex[0:nt_, :], in_=g0[0:nt_, 0:256], func=AF.Exp,
                                                               scale=-1.0), reads=[g0b], writes=[exb])
                    fw.op(fw.act, lambda: nc.scalar.activation(out=nl[0:nt_, :], in_=ex[0:nt_, :], func=AF.Ln,
                                                               bias=self.oneT[0:nt_, 0:1]),
                          reads=[exb, self.cb], writes=[nlb])
                    if "cut2" in DBG:
                        continue
                    self.mm(g0[0:nt_, 256:512], tri_f[0:nt_, 0:nt_], nl[0:nt_, :], True, True, [nlb, self.cb], [g0b])
                    for g in range(2):
                        self.mm(g1[:, g, 0:nt_], nl[0:nt_, g * 128:(g + 1) * 128], tri_f[0:nt_, 0:nt_], True, True,
                                [nlb, self.cb], [g1b], inc=(g == 1))
                    fw.op(fw.act, lambda: nc.scalar.activation(out=ektok[0:nt_, :], in_=g0[0:nt_, 256:512],
                                                               func=AF.Exp, scale=1.0 / 16.0),
                          reads=[g0b], writes=[ektb])
                    fw.op(fw.act, lambda: nc.scalar.activation(out=eqT[:, :, 0:nt_], in_=g1[:, :, 0:nt_],
                                                               func=AF.Exp, scale=-1.0 / 16.0),
                          reads=[g1b], writes=[eqb])
                    fw.op(fw.act, lambda: nc.scalar.activation(out=ekT[:, :, 0:nt_], in_=g1[:, :, 0:nt_],
                                                               func=AF.Exp, scale=1.0 / 16.0),
                          reads=[g1b], writes=[ekb])
                    if "cut3" in DBG:
                        continue
                    if sample:
                        fw.op(fw.dve, lambda: nc.vector.tensor_tensor(out=kttok[0:nt_, :], in0=pk[0:nt_, 0:256],
                                                                      in1=ektok[0:nt_, :], op=ALU.mult),
                              reads=[pkb, ektb], writes=[kttb])
                    else:
                        for j in range(2):
                            fw.op(fw.dve, lambda: nc.vector.tensor_tensor(
                                out=kbd[j * 64:(j + 1) * 64, j, :], in0=pk[j * 64:(j + 1) * 64, 0:256],
                                in1=ektok[j * 64:(j + 1) * 64, :], op=ALU.mult),
                                  reads=[pkb, ektb], writes=[kbdb])
                    pv, pvb = pj.next()
                    for kc in range(KC):
                        self.mm(pv[0:nt_, :], ht[:, kc, s0:s0 + nt_], w_g[:, kc, 512:1024], kc == 0, kc == KC - 1,
                                [wb, hb], [pvb])
                    fw.op(fw.act, lambda: nc.scalar.copy(out=vtok[0:nt_, :], in_=pv[0:nt_, :]), reads=[pvb], writes=[vtb])
                    for two in range(2):
                        ps_ = slice(two * 64, (two + 1) * 64)
                        fw.op(fw.dve, lambda: nc.vector.scalar_tensor_tensor(
                            out=qbd[ps_, :, two, 0:nt_], in0=qf[ps_, :, s0:s0 + nt_], scalar=0.125,
                            in1=eqT[ps_, :, 0:nt_], op0=ALU.mult, op1=ALU.mult), reads=[qfb, eqb], writes=[qtb])
                    fw.op(fw.dve, lambda: nc.vector.tensor_tensor(out=ktT[:, :, 0:nt_], in0=kf[:, :, s0:s0 + nt_],
                                                                  in1=ekT[:, :, 0:nt_], op=ALU.mult),
                          reads=[kfb, ekb], writes=[ktb])
                    if "cut35" in DBG:
                        continue
                    for g in range(2):
                        if nt_ == 128:
                            self.mm(g2[0:nt_, 2 * g:2 * g + 2, 0:nt_], ktT[:, g, 0:nt_], qbd[:, g, :, 0:nt_], True, True,
                                    [ktb, qtb], [g2b], inc=(g == 1))
                        else:
                            for two in range(2):
                                self.mm(g2[0:nt_, 2 * g + two, 0:nt_], ktT[:, g, 0:nt_], qbd[:, g, two, 0:nt_],
                                        True, True, [ktb, qtb], [g2b], inc=(g == 1 and two == 1))
                    for hh in range(4):
                        fw.op(fw.dve, lambda: nc.vector.tensor_tensor(out=am[0:nt_, hh, 0:nt_], in0=g2[0:nt_, hh, 0:nt_],
                                                                      in1=tri_f[0:nt_, 0:nt_], op=ALU.mult),
                              reads=[g2b, self.cb], writes=[amb])
                    if "cut4" in DBG:
                        continue
                    if not sample:
                        for j in range(2):
                            for hh in range(4):
                                g, pb = hh // 2, (hh % 2) * 64
                                self.mm(g4[pb:pb + 64, j, g, :], kbd[:, j, hh * 64:(hh + 1) * 64],
                                        vtok[:, hh * 128:(hh + 1) * 128], True, True,
                                        [kbdb, vtb], [g4b], inc=(j == 1 and hh == 3))
                        starts = [(Sbf, Sbfb)]
                        for j in range(2):
                            nS, nSb = Sring.next()
                            fw.op(fw.dve, lambda: nc.vector.tensor_tensor(out=stmp[:, 0, :, :], in0=g4[:, j, :, :],
                                                                          in1=S[:, :, :], op=ALU.add),
                                  reads=[g4b, Sb], writes=[stmpb])
                            for g in range(2):
                                col = j * 64 + 63
                                fw.op(fw.dve, lambda: nc.vector.tensor_scalar(
                                    out=S[:, g, :], in0=stmp[:, 0, g, :], scalar1=eqT[:, g, col:col + 1], scalar2=None,
                                    op0=ALU.mult), reads=[stmpb, eqb], writes=[Sb])
                                fw.op(fw.act, lambda: nc.scalar.activation(
                                    out=nS[:, g, :], in_=stmp[:, 0, g, :], func=AF.Copy, scale=eqT[:, g, col:col + 1]),
                                      reads=[stmpb, eqb], writes=[nSb])
                            starts.append((nS, nSb))
                        for hh in range(4):
                            g, pb = hh // 2, (hh % 2) * 64
                            self.mm(g3[:, hh, 0:128], vtok[:, hh * 128:(hh + 1) * 128], am[:, hh, :], True, False,
                                    [vtb, amb], [g3b], inc=False)
                            for j in range(2):
                                sj, sjb = starts[j]
                                self.mm(g3[:, hh, j * 64:(j + 1) * 64], sj[:, g, :],
                                        qbd[:, g, hh % 2, j * 64:(j + 1) * 64], False, j == 1,
                                        [sjb, qtb], [g3b], inc=(j == 1))
                        Sbf, Sbfb = starts[2]
                        o_ap, o_b = g3[:, :, 0:128], g3b
                    else:
                        for hh in range(4):
                            self.mm(g3[:, hh, 0:NS], vtok[0:NS, hh * 128:(hh + 1) * 128], am[0:NS, hh, 0:NS], True, True,
                                    [vtb, amb], [g3b], inc=(hh == 3))
                        px = g1[:, 0, :].rearrange("p (h c) -> p h c", h=4)
                        for r in range(NSEQ // 2):
                            sbf, sbfb = stSbf.next()
                            fw.op(fw.pool, lambda: nc.gpsimd.tensor_copy(out=sbf[:], in_=stS[:, 2 * r:2 * r + 2, :, :]),
                                  reads=[stSb], writes=[sbfb])
                            kms = []
                            for b2 in range(2):
                                b = 2 * r + b2
                                km, kmb = kmask.next()
                                fw.op(fw.pool, lambda: nc.gpsimd.tensor_scalar(
                                    out=km[0:NS, :], in0=kttok[0:NS, :], scalar1=self.selb[0:NS, b:b + 1], scalar2=None,
                                    op0=ALU.mult), reads=[kttb, self.cb], writes=[kmb])
                                kms.append((km, kmb))
                            for b2 in range(2):
                                b = 2 * r + b2
                                km, kmb = kms[b2]
                                for hh in range(4):
                                    g, pb = hh // 2, (hh % 2) * 64
                                    self.mm(px[:, hh, 4 * b:4 * b + 4], sbf[:, b2, g, :],
                                            qbd[:, g, hh % 2, 4 * b:4 * b + 4], True, True, [sbfb, qtb], [g1b], inc=False)
                                    self.mm(g4[pb:pb + 64, b2, g, :], km[0:NS, hh * 64:(hh + 1) * 64],
                                            vtok[0:NS, hh * 128:(hh + 1) * 128], True, True, [kmb, vtb], [g4b],
                                            inc=(hh == 3))
                            fw.op(fw.dve, lambda: nc.vector.tensor_tensor(out=stmp[:], in0=g4[:],
                                                                          in1=stS[:, 2 * r:2 * r + 2, :, :], op=ALU.add),
                                  reads=[g4b, stSb], writes=[stmpb])
                            for b2 in range(2):
                                b = 2 * r + b2
                                for g in range(2):
                                    col = 4 * b + 3
                                    fw.op(fw.dve, lambda: nc.vector.tensor_scalar(
                                        out=stS[:, b, g, :], in0=stmp[:, b2, g, :], scalar1=eqT[:, g, col:col + 1],
                                        scalar2=None, op0=ALU.mult), reads=[stmpb, eqb], writes=[stSb])
                        fw.op(fw.act, lambda: nc.scalar.copy(out=oin[:], in_=px), reads=[g1b], writes=[oinb])
                        fw.op(fw.dve, lambda: nc.vector.tensor_tensor(out=of[:, :, 0:NS], in0=g3[:, :, 0:NS],
                                                                      in1=oin[:], op=ALU.add),
                              reads=[g3b, oinb], writes=[ofb])
                        o_ap, o_b = of[:, :, 0:NS], ofb
                        gs_out = d["gla_s"][i].rearrange("b (g two) dd e -> two dd b g e", two=2)
                        for two in range(2):
                            fw.dma(fw.sp, gs_out[two], stS[two * 64:(two + 1) * 64, :, :, :], reads=[stSb])
                    if "cut5" in DBG:
                        continue
                    def v4(tl):
                        return tl[:].rearrange("p h c -> p (h c)")[:, 0:4 * nt_].rearrange("p (h c) -> p h c", h=4)
                    osq_v, ors_v, otm_v = v4(osq), v4(ors), v4(otm)
                    fw.op(fw.act, lambda: nc.scalar.activation(out=osq_v, in_=o_ap, func=AF.Square),
                          reads=[o_b], writes=[osqb])
                    pn, pnb = pj.next()
                    pnv = pn[:, 0:4 * nt_].rearrange("p (h c) -> p h c", h=4)
                    self.mm(pnv, self.ones_bf[:], osq_v, True, True, [osqb, self.cb], [pnb])
                    self.rstd_from_ssq(pnv, 128, ors_v, otm_v, [pnb], orsb, otmb)
                    fw.op(fw.dve, lambda: nc.vector.scalar_tensor_tensor(
                        out=otm_v, in0=o_ap, scalar=L.prm[:, 0:1], in1=ors_v,
                        op0=ALU.mult, op1=ALU.mult), reads=[o_b, orsb, L.prmb], writes=[otmb])
                    fw.op(fw.dve, lambda: nc.vector.tensor_tensor(
                        out=L.og[:, :, t0 + s0:t0 + s0 + nt_], in0=otm_v, in1=sg[:, :, s0:s0 + nt_],
                        op=ALU.mult), reads=[otmb, sgb], writes=[L.ogb])
                if t0 + n == SEQ:
                    gp_out = d["gla_p"][i].rearrange("(g two) dd e -> two dd g e", two=2)
                    for two in range(2):
                        fw.dma(fw.sp, gp_out[two], S[two * 64:(two + 1) * 64, :, :], reads=[Sb])
            fw.barrier()

    def ab_phase_b(self, L):
        nc, fw, d = self.nc, self.fw, self.dram
        i = L.i
        with ExitStack() as es:
            sb = lambda n, s, dt: es.enter_context(_sbuf(nc, n, s, dt))
            w_uq = sb("w_uq_b", [128, 2, 768], BF16); wqb = Buf("w_uq")
            w_sw = sb("w_uqsw_b", [128, 2, 8, 32], BF16); wsb = Buf("w_uqsw")
            w_kv = sb("w_ukv_b", [128, 2, 1024], BF16); wkb = Buf("w_ukv")
            with ExitStack() as es2:
                st = es2.enter_context(_sbuf(nc, "bstg", [128, 2, 1024], F32)); stb = Buf("bstg")
                fw.dma(fw.sp, st[:, :, 0:768], d["w_uq"][i].rearrange("(kc p) n -> p kc n", p=128), writes=[stb])
                fw.op(fw.pool, lambda: nc.gpsimd.tensor_copy(out=w_uq[:], in_=st[:, :, 0:768]), reads=[stb], writes=[wqb])
                sv = st[:, :, 0:768].rearrange("p k (h f) -> p k h f", h=8)
                fw.op(fw.pool, lambda: nc.gpsimd.tensor_copy(out=w_sw[:, :, :, 0:16], in_=sv[:, :, :, 80:96]),
                      reads=[stb], writes=[wsb])
                fw.op(fw.pool, lambda: nc.gpsimd.tensor_copy(out=w_sw[:, :, :, 16:32], in_=sv[:, :, :, 64:80]),
                      reads=[stb], writes=[wsb])
                fw.dma(fw.sp, st[:], d["w_ukv"][i].rearrange("(kc p) n -> p kc n", p=128), reads=[stb], writes=[stb])
                fw.op(fw.pool, lambda: nc.gpsimd.tensor_copy(out=w_kv[:], in_=st[:]), reads=[stb], writes=[wkb])
                fw.barrier()
            w_ukT = sb("w_ukT_b", [64, 8, 256], BF16); wtb = Buf("w_ukT")
            with ExitStack() as es2:
                st = es2.enter_context(_sbuf(nc, "bstg2", [64, 8, 256], F32)); stb = Buf("bstg2")
                fw.dma(fw.sp, st[:], d["w_ukT"][i], writes=[stb])
                fw.op(fw.pool, lambda: nc.gpsimd.tensor_copy(out=w_ukT[:], in_=st[:]), reads=[stb], writes=[wtb])
                fw.barrier()
            qlat = sb("qlat", [128, 2, NSEQ, 8, 4], BF16); qlb = Buf("qlat")
            qpeS = sb("qpeS", [128, NSEQ, 8, 4], BF16); qpb = Buf("qpeS")
            fw.op(fw.dve, lambda: nc.vector.memset(qpeS[:], 0.0), writes=[qpb])
            esA = ExitStack()
            es_outer, es = es, esA
            sb = lambda n, s, dt: esA.enter_context(_sbuf(nc, n, s, dt))
            Qr = Ring(nc, es, "Qh", 2, [96, NT], BF16)
            Vt = [sb(f"Vh{j}", [128, 16, 128], BF16) for j in range(2)]
            Vb = [Buf(f"Vh{j}") for j in range(2)]
            for j in range(2):
                fw.op(fw.pool, lambda: nc.gpsimd.memset(Vt[j][:], 1.0), writes=[Vb[j]])
            pa = Ring(nc, es, "pa", 1, [128, 2, 512], F32, psum=True)
            Sp = Ring(nc, es, "Sp", 3, [128, 512], F32, psum=True)
            Op = Ring(nc, es, "Op", 2, [128, 512], F32, psum=True)
            pt = Ring(nc, es, "ptl", 3, [128, 512], BF16)
            r1 = sb("r1", [96, 512], F32); r1b = Buf("r1")
            r2 = sb("r2", [96, 512], F32); r2b = Buf("r2")
            rc = Ring(nc, es, "rc", 2, [128, 512], F32)
            for hh in range(8):
                par = hh % 2
                Q, Qb = Qr.next()
                K, Kb = L.K[par], L.Kb[par]
                V, Vbuf = Vt[par], Vb[par]
                voff = 0 if par == 0 else 64
                doff = 64 - voff
                for t, (t0, n) in enumerate(TILES):
                    p, pb_ = pa.next()
                    for kc in range(2):
                        self.mm(p[0:96, 0, :n], w_uq[:, kc, hh * 96:(hh + 1) * 96], L.cqT[:, kc, t0:t0 + n],
                                kc == 0, kc == 1, [wqb, L.cqb], [pb_])
                    for kc in range(2):
                        self.mm(p[64:96, 1, :n], w_sw[:, kc, hh, :], L.cqT[:, kc, t0:t0 + n],
                                kc == 0, kc == 1, [wsb, L.cqb], [pb_])
                    fw.op(fw.act, lambda: nc.scalar.copy(out=Q[0:64, t0:t0 + n], in_=p[0:64, 0, :n]),
                          reads=[pb_], writes=[Qb])
                    fw.op(fw.dve, lambda: nc.vector.tensor_tensor(out=r1[64:96, :n], in0=p[64:96, 0, :n],
                                                                  in1=self.ropeT[64:96, t0:t0 + n], op=ALU.mult),
                          reads=[pb_, self.cb], writes=[r1b])
                    fw.op(fw.dve, lambda: nc.vector.tensor_tensor(out=r2[64:96, :n], in0=p[64:96, 1, :n],
                                                                  in1=self.ropeT[32:64, t0:t0 + n], op=ALU.mult),
                          reads=[pb_, self.cb], writes=[r2b])
                    fw.op(fw.dve, lambda: nc.vector.tensor_tensor(out=Q[64:96, t0:t0 + n], in0=r1[64:96, :n],
                                                                  in1=r2[64:96, :n], op=ALU.add),
                          reads=[r1b, r2b], writes=[Qb])
                p, pb_ = pa.next()
                for kc in range(2):
                    self.mm(p[:, kc, 0:NS], w_ukT[0:64, hh, kc * 128:(kc + 1) * 128], Q[0:64, SEQ:NT], True, True,
                            [wtb, Qb], [pb_], inc=(kc == 1))
                for kc in range(2):
                    fw.op(fw.act, lambda: nc.scalar.copy(out=qlat[:, kc, :, hh, :],
                                                         in_=p[:, kc, 0:NS].rearrange("p (b t) -> p b t", t=4)),
                          reads=[pb_], writes=[qlb])
                fw.op(fw.act, lambda: nc.scalar.copy(out=qpeS[0:32, :, hh, :],
                                                     in_=Q[64:96, SEQ:NT].rearrange("p (b t) -> p b t", t=4)),
                      reads=[Qb], writes=[qpb])
                for t, (t0, n) in enumerate(TILES[:4]):
                    p, pb_ = pa.next()
                    for kc in range(2):
                        self.mm(p[0:64, 0, :n], w_kv[:, kc, hh * 128:hh * 128 + 64], L.ckvT[:, kc, t0:t0 + n],
                                kc == 0, kc == 1, [wkb, L.ckvb], [pb_])
                    fw.op(fw.act, lambda: nc.scalar.copy(out=K[0:64, t0:t0 + n], in_=p[0:64, 0, :n]),
                          reads=[pb_], writes=[Kb])
                for half in range(2):
                    p, pb_ = pa.next()
                    pv = p[:, 0, :].rearrange("p (j c) -> p j c", j=8)
                    for jb in range(8):
                        sblk = half * 8 + jb
                        for kc in range(2):
                            self.mm(pv[:, jb, :], L.ckvT[:, kc, sblk * 128:(sblk + 1) * 128],
                                    w_kv[:, kc, hh * 128 + 64:hh * 128 + 128], kc == 0, kc == 1,
                                    [wkb, L.ckvb], [pb_], inc=(kc == 1 and jb == 7))
                    fw.op(fw.act, lambda: nc.scalar.copy(out=V[:, half * 8:half * 8 + 8, voff:voff + 64], in_=pv),
                          reads=[pb_], writes=[Vbuf])
                if "noattn" in DBG:
                    continue
                for qt in range(4):
                    o, ob = Op.next()
                    last = 4 * qt + 3
                    for j in range(last + 1):
                        n0 = max(0, j - 4 * qt) * 128
                        sp, spb = Sp.next()
                        self.mm(sp[:, n0:512], K[0:96, j * 128:(j + 1) * 128], Q[0:96, qt * 512 + n0:(qt + 1) * 512],
                                True, True, [Kb, Qb], [spb])
                        ptile, ptb = pt.next()
                        fw.op(fw.act, lambda: nc.scalar.activation(out=ptile[:, n0:512], in_=sp[:, n0:512], func=AF.Exp,
                                                                   scale=MLA_SCALE), reads=[spb], writes=[ptb])
                        if j >= 4 * qt:
                            fw.op(fw.dve, lambda: nc.vector.tensor_tensor(
                                out=ptile[:, n0:n0 + 128], in0=ptile[:, n0:n0 + 128], in1=self.tri_b[:], op=ALU.mult),
                                  reads=[ptb, self.cb], writes=[ptb])
                        self.mm(o[:, n0:512], V[:, j, :], ptile[:, n0:512], j == 0, j == last, [Vbuf, ptb], [ob],
                                inc=True)
                    rct, rcb = rc.next()
                    fw.op(fw.dve, lambda: nc.vector.reciprocal(out=rct[doff:doff + 64, :], in_=o[doff:doff + 64, :]),
                          reads=[ob], writes=[rcb])
                    fw.op(fw.dve, lambda: nc.vector.tensor_tensor(
                        out=L.om[voff:voff + 64, hh // 2, qt * 512:(qt + 1) * 512], in0=o[voff:voff + 64, :],
                        in1=rct[doff:doff + 64, :], op=ALU.mult), reads=[ob, rcb], writes=[L.omb])
            fw.barrier()
            esA.close()
            es = es_outer
            sb = lambda n, s, dt: es.enter_context(_sbuf(nc, n, s, dt))
            if "nocache" in DBG:
                fw.op(fw.pool, lambda: nc.gpsimd.memset(L.om[:, :, SEQ:NT], 0.0), writes=[L.omb])
                fw.barrier()
                return
            self.mla_cached(L, es, w_kv, wkb, qlat, qlb, qpeS, qpb)
            fw.barrier()

    def mla_cached(self, L, es, w_kv, wkb, qlat, qlb, qpeS, qpb):
        nc, fw, d = self.nc, self.fw, self.dram
        i = L.i
        sb = lambda n, s, dt: es.enter_context(_sbuf(nc, n, s, dt))
        G = 4
        CW = 290
        pgb = Ring(nc, es, "pgb", 4, [128, G, CW], BF16)
        for t_, b_ in zip(pgb.tiles, pgb.bufs):
            fw.op(fw.dve, lambda: nc.vector.memset(t_[:], 1.0), writes=[b_])
        ct = Ring(nc, es, "ctT", 2, [128, 2, G * 128], BF16)
        rt = Ring(nc, es, "rtT", 2, [128, G * 128], BF16)
        for t_, b_ in zip(rt.tiles, rt.bufs):
            fw.op(fw.dve, lambda: nc.vector.memset(t_[:], 0.0), writes=[b_])
        pT = Ring(nc, es, "pT", 3, [128, G, 32], BF16)
        trp = Ring(nc, es, "trp", 2, [128, 2, G * 128], BF16, psum=True)
        trr = Ring(nc, es, "trr", 2, [128, 1024], BF16, psum=True)
        sc = Ring(nc, es, "scp", 2, [128, 512], F32, psum=True)
        acc = Ring(nc, es, "accp", 2, [128, 512], F32, psum=True)
        kpeS = sb("kpeS", [32, NS], BF16); kpb = Buf("kpeS")
        cnew = sb("cnew", [64, 258], BF16); cnb = Buf("cnew")
        olT = sb("olT", [128, 2, NSEQ, 32], BF16); olb = Buf("olT")
        pnf = sb("pnf", [64, 32], F32); pnfb = Buf("pnf")
        pnb_t = Ring(nc, es, "pnb", 2, [64, 32], BF16)
        rcp = Ring(nc, es, "rcp", 2, [32, 1], F32)
        olat = Ring(nc, es, "olat", 2, [32, 256], BF16)
        fw.op(fw.act, lambda: nc.scalar.copy(out=kpeS[0:32, :], in_=L.K[0][64:96, SEQ:NT]), reads=[L.Kb[0]], writes=[kpb])
        fw.op(fw.pool, lambda: nc.gpsimd.memset(cnew[:], 1.0), writes=[cnb])
        tp, tpb = trr.next()
        for kc in range(2):
            fw.op(fw.pe, lambda: nc.tensor.transpose(out=tp[0:NS, kc * 128:(kc + 1) * 128], in_=L.ckvT[:, kc, SEQ:NT],
                                                     identity=self.ident_b[:]),
                  reads=[L.ckvb, self.cb], writes=[tpb], inc=(kc == 1))
        fw.op(fw.act, lambda: nc.scalar.copy(out=cnew[0:NS, 0:256], in_=tp[0:NS, 0:256]), reads=[tpb], writes=[cnb])
        ckv_rows = d["cache_kv"].rearrange("l n p f -> (l n p) f")
        ckr_rows = d["cache_kr"].rearrange("l n p f -> (l n p) f")
        if i == 0:
            idx_t, idxb = self.ptab, self.ptb
        else:
            idx_t = sb("idx_l", [128, NSEQ * NPAGES], I32); idxb = Buf("idx_l")
            fw.op(fw.dve, lambda: nc.vector.tensor_scalar(out=idx_t[:], in0=self.ptab[:], scalar1=float(i * NPOOL * 128),
                                                          scalar2=None, op0=ALU.add),
                  reads=[self.ptb], writes=[idxb])
        for b in range(NSEQ):
            a, ab = acc.next()
            for gi in range(NPAGES // G):
                pb16, pbb = pgb.next()
                for k in range(G):
                    pg = b * NPAGES + gi * G + k
                    fw.gather(pb16[:, k, 0:256], ckv_rows, idx_t[:, pg:pg + 1], reads=[idxb], writes=[pbb])
                    fw.gather(pb16[:, k, 257:289], ckr_rows, idx_t[:, pg:pg + 1], reads=[idxb], writes=[pbb])
                tc_, tcb = trp.next()
                tr_, trb = trr.next()
                for k in range(G):
                    for kc in range(2):
                        fw.op(fw.pe, lambda: nc.tensor.transpose(out=tc_[:, kc, k * 128:(k + 1) * 128],
                                                                 in_=pb16[:, k, kc * 128:(kc + 1) * 128],
                                                                 identity=self.ident_b[:]),
                              reads=[pbb, self.cb], writes=[tcb], inc=(k == G - 1 and kc == 1))
                for k in range(G):
                    fw.op(fw.pe, lambda: nc.tensor.transpose(out=tr_[0:32, k * 128:(k + 1) * 128], in_=pb16[:, k, 257:289],
                                                             identity=self.ident_b[:]),
                          reads=[pbb, self.cb], writes=[trb], inc=(k == G - 1))
                c_, cb_ = ct.next()
                r_, rb_ = rt.next()
                fw.op(fw.dve, lambda: nc.vector.tensor_copy(out=c_[:], in_=tc_[:]), reads=[tcb], writes=[cb_])
                fw.op(fw.act, lambda: nc.scalar.copy(out=r_[0:32, :], in_=tr_[0:32, 0:G * 128]), reads=[trb], writes=[rb_])
                s_, sb_ = sc.next()
                for k in range(G):
                    o_ = s_[:, k * 32:(k + 1) * 32]
                    self.mm(o_, c_[:, 0, k * 128:(k + 1) * 128], qlat[:, 0, b, :, :].rearrange("p h t -> p (h t)"),
                            True, False, [cb_, qlb], [sb_], inc=False)
                    self.mm(o_, c_[:, 1, k * 128:(k + 1) * 128], qlat[:, 1, b, :, :].rearrange("p h t -> p (h t)"),
                            False, False, [cb_, qlb], [sb_], inc=False)
                    self.mm(o_, r_[:, k * 128:(k + 1) * 128], qpeS[:, b, :, :].rearrange("p h t -> p (h t)"),
                            False, True, [rb_, qpb], [sb_], inc=(k == G - 1))
                p_, pb_ = pT.next()
                fw.op(fw.act, lambda: nc.scalar.activation(out=p_[:].rearrange("p g q -> p (g q)"), in_=s_[:, 0:G * 32],
                                                           func=AF.Exp, scale=MLA_SCALE), reads=[sb_], writes=[pb_])
                for k in range(G):
                    self.mm(a[0:32, 0:257], p_[:, k, :], pb16[:, k, 0:257], gi == 0 and k == 0, False,
                            [pb_, pbb], [ab], inc=(k == G - 1))
            s_, sb_ = sc.next()
            qv = [qlat[:, kc, b, :, :].rearrange("p h t -> p (h t)") for kc in range(2)]
            self.mm(s_[0:NS, 0:32], L.ckvT[:, 0, SEQ:NT], qv[0], True, False, [L.ckvb, qlb], [sb_], inc=False)
            self.mm(s_[0:NS, 0:32], L.ckvT[:, 1, SEQ:NT], qv[1], False, False, [L.ckvb, qlb], [sb_], inc=False)
            self.mm(s_[0:NS, 0:32], kpeS[0:32, :], qpeS[0:32, b, :, :].rearrange("p h t -> p (h t)"), False, True,
                    [kpb, qpb], [sb_])
            fw.op(fw.act, lambda: nc.scalar.activation(out=pnf[:], in_=s_[0:NS, 0:32], func=AF.Exp, scale=MLA_SCALE),
                  reads=[sb_], writes=[pnfb])
            pn_, pnb_ = pnb_t.next()
            fw.op(fw.dve, lambda: nc.vector.tensor_tensor(out=pn_[:], in0=pnf[:], in1=self.maskq[:, b * 32:(b + 1) * 32],
                                                          op=ALU.mult), reads=[pnfb, self.cb], writes=[pnb_])
            self.mm(a[0:32, 0:257], pn_[0:NS, :], cnew[0:NS, 0:257], False, True, [pnb_, cnb], [ab])
            rc_, rcb_ = rcp.next()
            ol_, olb_ = olat.next()
            fw.op(fw.dve, lambda: nc.vector.reciprocal(out=rc_[:], in_=a[0:32, 256:257]), reads=[ab], writes=[rcb_])
            fw.op(fw.act, lambda: nc.scalar.activation(out=ol_[:], in_=a[0:32, 0:256], func=AF.Copy, scale=rc_[0:32, 0:1]),
                  reads=[ab, rcb_], writes=[olb_])
            tq, tqb = trr.next()
            for kc in range(2):
                fw.op(fw.pe, lambda: nc.tensor.transpose(out=tq[:, kc * 32:(kc + 1) * 32], in_=ol_[0:32, kc * 128:(kc + 1) * 128],
                                                         identity=self.ident_b[0:32, 0:32]),
                      reads=[olb_, self.cb], writes=[tqb], inc=(kc == 1))
            fw.op(fw.dve, lambda: nc.vector.tensor_copy(out=olT[:, :, b, :],
                                                        in_=tq[:, 0:64].rearrange("p (k q) -> p k q", k=2)),
                  reads=[tqb], writes=[olb])
        for hh in range(8):
            voff = 0 if hh % 2 == 0 else 64
            a, ab = acc.next()
            ov = a[voff:voff + 64, 0:NS].rearrange("p (b t) -> p b t", t=4)
            for kc in range(2):
                self.mm(ov, w_kv[:, kc, hh * 128 + 64:hh * 128 + 128], olT[:, kc, :, hh * 4:(hh + 1) * 4],
                        kc == 0, kc == 1, [wkb, olb], [ab])
            fw.op(fw.act, lambda: nc.scalar.copy(out=L.om[voff:voff + 64, hh // 2, SEQ:NT], in_=a[voff:voff + 64, 0:NS]),
                  reads=[ab], writes=[L.omb])

    def ab_phase_c(self, L):
        nc, fw, d = self.nc, self.fw, self.dram
        with ExitStack() as es:
            w_o, wb = self.load_w_cols(es, d["w_out_ab"][L.i], 0, D, "w_o")
            ps = Ring(nc, es, "pc", 4, [128, 512], F32, psum=True)
            for t, (t0, n) in enumerate(TILES):
                for m in range(KC):
                    p, pb_ = ps.next()
                    for kc in range(KC):
                        src, sbuf_ = (L.og, L.ogb) if kc < 4 else (L.om, L.omb)
                        self.mm(p[:, :n], w_o[:, kc, m * 128:(m + 1) * 128], src[:, kc % 4, t0:t0 + n],
                                kc == 0, kc == KC - 1, [wb, sbuf_], [pb_])
                    fw.op(fw.dve, lambda: nc.vector.tensor_tensor(out=self.x[:, m, t0:t0 + n], in0=p[:, :n],
                                                                  in1=self.x[:, m, t0:t0 + n], op=ALU.add),
                          reads=[pb_, self.xb[t]], writes=[self.xb[t]])
            fw.barrier()

    def mlp(self, l):
        nc, fw, d = self.nc, self.fw, self.dram
        with ExitStack() as es:
            sb = lambda n, s, dt: es.enter_context(_sbuf(nc, n, s, dt))
            h = sb("h", [128, KC, NT], BF16)
            hb = [Buf(f"h{t}") for t in range(5)]
            with ExitStack() as es2:
                R = self.norm_rings(es2, 512, nps=2)
                for t, (t0, n) in enumerate(TILES):
                    self.norm_tile(R, self.gmlp[:, l, :], t, t0, n, h[:, :, t0:t0 + n], hb[t])
                fw.barrier()
            stg = Ring(nc, es, "mstg", 2, [128, 4096], F32)
            wu = Ring(nc, es, "wu", 2, [128, KC, 512], BF16)
            wd = Ring(nc, es, "wd", 2, [128, 4, D], BF16)
            h1 = Ring(nc, es, "h1", 2, [128, 4, 512], BF16)
            sq = Ring(nc, es, "msq", 2, [128, 512], F32)
            pu = Ring(nc, es, "pu", 4, [128, 512], F32, psum=True)
            pd = Ring(nc, es, "pd", 4, [128, 512], F32, psum=True)
            wuv = d["w_up"][l].rearrange("(kc p) n -> p kc n", p=128)
            wdv = d["w_down"][l].rearrange("(c kc p) n -> c p kc n", p=128, kc=4)
            for c in range(DFF // 512):
                s1, s1b = stg.next()
                s1v = s1[:].rearrange("p (kc n) -> p kc n", kc=KC)
                for kh in range(2):
                    fw.dma(fw.sp, s1v[:, 4 * kh:4 * kh + 4, :], wuv[:, 4 * kh:4 * kh + 4, c * 512:(c + 1) * 512], writes=[s1b])
                wut, wub = wu.next()
                fw.op(fw.pool, lambda: nc.gpsimd.tensor_copy(out=wut[:], in_=s1v), reads=[s1b], writes=[wub])
                s2, s2b = stg.next()
                s2v = s2[:].rearrange("p (kc n) -> p kc n", kc=4)
                for kh in range(2):
                    fw.dma(fw.sp, s2v[:, 2 * kh:2 * kh + 2, :], wdv[c][:, 2 * kh:2 * kh + 2, :], writes=[s2b])
                wdt, wdb = wd.next()
                fw.op(fw.pool, lambda: nc.gpsimd.tensor_copy(out=wdt[:], in_=s2v), reads=[s2b], writes=[wdb])
                for t, (t0, n) in enumerate(TILES):
                    h1t, h1b = h1.next()
                    for m4 in range(4):
                        p, pb_ = pu.next()
                        for kc in range(KC):
                            self.mm(p[:, :n], wut[:, kc, m4 * 128:(m4 + 1) * 128], h[:, kc, t0:t0 + n],
                                    kc == 0, kc == KC - 1, [wub, hb[t]], [pb_])
                        sqt, sqb = sq.next()
                        fw.op(fw.act, lambda: nc.scalar.activation(out=sqt[:, :n], in_=p[:, :n], func=AF.Square),
                              reads=[pb_], writes=[sqb])
                        fw.op(fw.dve, lambda: nc.vector.scalar_tensor_tensor(
                            out=h1t[:, m4, :n], in0=p[:, :n], scalar=0.0, in1=sqt[:, :n], op0=ALU.is_gt, op1=ALU.mult),
                              reads=[pb_, sqb], writes=[h1b])
                    for m in range(KC):
                        p, pb_ = pd.next()
                        for k4 in range(4):
                            self.mm(p[:, :n], wdt[:, k4, m * 128:(m + 1) * 128], h1t[:, k4, :n],
                                    k4 == 0, k4 == 3, [wdb, h1b], [pb_])
                        fw.op(fw.dve, lambda: nc.vector.tensor_tensor(out=self.x[:, m, t0:t0 + n], in0=p[:, :n],
                                                                      in1=self.x[:, m, t0:t0 + n], op=ALU.add),
                              reads=[pb_, self.xb[t]], writes=[self.xb[t]])
            fw.barrier()

    def conv_layer(self, i, l):
        nc, fw, d = self.nc, self.fw, self.dram
        PADW = 30
        with ExitStack() as es:
            sb = lambda n, s, dt: es.enter_context(_sbuf(nc, n, s, dt))
            prm = sb("cvprm", [128, 48], F32); prmb = Buf("cvprm")
            wdw = sb("wdw", [128, KC, 31], F32); wdwb = Buf("wdw")
            ld = lambda dst, src, pat: fw.dma(fw.sp, dst, src.rearrange(pat, p=128), writes=[prmb],
                                              allow_slow_non_contiguous=True)
            ld(prm[:, 0:16], d["b_pw1"][i:i + 1, :], "o (m p) -> p (o m)")
            ld(prm[:, 16:24], d["b_dw"][i:i + 1, :], "o (m p) -> p (o m)")
            ld(prm[:, 24:32], d["conv_ln_g"][i:i + 1, :], "o (m p) -> p (o m)")
            ld(prm[:, 32:40], d["conv_ln_b"][i:i + 1, :], "o (m p) -> p (o m)")
            ld(prm[:, 40:48], d["b_pw2"][i:i + 1, :], "o (m p) -> p (o m)")
            fw.dma(fw.sp, wdw[:], d["w_dwT"][i].rearrange("(kc p) j -> p kc j", p=128), writes=[wdwb])
            ugp = sb("ugp", [128, KC, PADW + SEQ], BF16); ugpb = Buf("ugp")
            ugs = sb("ugs", [128, KC, NSEQ, 34], BF16); ugsb = Buf("ugs")
            fw.op(fw.pool, lambda: nc.gpsimd.memset(ugp[:, :, 0:PADW], 0.0), writes=[ugpb])
            with ExitStack() as e1:
                s1 = lambda n, s_, dt: e1.enter_context(_sbuf(nc, n, s_, dt))
                h = s1("h", [128, KC, NT], BF16)
                hb = [Buf(f"h{t}") for t in range(5)]
                with ExitStack() as e2:
                    R = self.norm_rings(e2, 512, nps=2)
                    for t, (t0, n) in enumerate(TILES):
                        self.norm_tile(R, self.gmix[:, l, :], t, t0, n, h[:, :, t0:t0 + n], hb[t])
                    fw.barrier()
                E = s1("cvE", [128, KC, NSEQ, 34], F32); Eb = Buf("cvE")
                cpo = s1("cpo", [128, KC, 30], F32); cpob = Buf("cpo")
                scv = d["state_conv"][i].rearrange("(kc p) b j -> p kc b j", p=128)
                for kc in range(KC):
                    fw.dma(fw.sp, E[:, kc, :, 0:30], scv[:, kc, :, :], writes=[Eb])
                stg = Ring(nc, e1, "p1stg", 1, [128, KC, 256], F32)
                wch = Ring(nc, e1, "p1w", 2, [128, KC, 256], BF16)
                pv = Ring(nc, e1, "p1v", 3, [128, 512], F32, psum=True)
                pg = Ring(nc, e1, "p1g", 3, [128, 512], F32, psum=True)
                sig = Ring(nc, e1, "p1sig", 1, [128, 512], F32)
                uf = Ring(nc, e1, "p1uf", 1, [128, 512], F32)
                w1v = d["w_pw1"][i].rearrange("(kc p) n -> p kc n", p=128)
                for m in range(KC):
                    st, stb = stg.next()
                    fw.dma(fw.sp, st[:, :, 0:128], w1v[:, :, m * 128:(m + 1) * 128], writes=[stb])
                    fw.dma(fw.sp, st[:, :, 128:256], w1v[:, :, D + m * 128:D + (m + 1) * 128], writes=[stb])
                    w, wb = wch.next()
                    fw.op(fw.pool, lambda: nc.gpsimd.tensor_copy(out=w[:], in_=st[:]), reads=[stb], writes=[wb])
                    for t, (t0, n) in enumerate(TILES):
                        a, ab = pv.next()
                        g, gb = pg.next()
                        for kc in range(KC):
                            self.mm(a[:, :n], w[:, kc, 0:128], h[:, kc, t0:t0 + n], kc == 0, kc == KC - 1, [wb, hb[t]], [ab])
                        for kc in range(KC):
                            self.mm(g[:, :n], w[:, kc, 128:256], h[:, kc, t0:t0 + n], kc == 0, kc == KC - 1, [wb, hb[t]], [gb])
                        sg_, sgb_ = sig.next()
                        u, ub = uf.next()
                        fw.op(fw.act, lambda: nc.scalar.activation(out=sg_[:, :n], in_=g[:, :n], func=AF.Sigmoid,
                                                                   bias=prm[:, 8 + m:9 + m]),
                              reads=[gb, prmb], writes=[sgb_])
                        fw.op(fw.dve, lambda: nc.vector.scalar_tensor_tensor(
                            out=u[:, :n], in0=a[:, :n], scalar=prm[:, m:m + 1], in1=sg_[:, :n], op0=ALU.add, op1=ALU.mult),
                              reads=[ab, sgb_, prmb], writes=[ub])
                        if t < 4:
                            fw.op(fw.pool, lambda: nc.gpsimd.tensor_copy(out=ugp[:, m, PADW + t0:PADW + t0 + n], in_=u[:, :n]),
                                  reads=[ub], writes=[ugpb])
                            if t == 3:
                                fw.op(fw.pool, lambda: nc.gpsimd.tensor_copy(out=cpo[:, m, :], in_=u[:, n - 30:n]),
                                      reads=[ub], writes=[cpob])
                        else:
                            uv = u[:, 0:NS].rearrange("p (b j) -> p b j", j=4)
                            fw.op(fw.pool, lambda: nc.gpsimd.tensor_copy(out=E[:, m, :, 30:34], in_=uv),
                                  reads=[ub], writes=[Eb])
                fw.dma(fw.sp, d["conv_p"][i].rearrange("(kc p) j -> p kc j", p=128), cpo[:], reads=[cpob])
                fw.op(fw.pool, lambda: nc.gpsimd.tensor_copy(out=ugs[:], in_=E[:]), reads=[Eb], writes=[ugsb])
                cov = d["conv_s"][i].rearrange("(kc p) b j -> p kc b j", p=128)
                for kc in range(KC):
                    fw.dma(fw.sp, cov[:, kc, :, :], E[:, kc, :, 4:34], reads=[Eb])
                fw.barrier()
            if "convp1" in DBG:
                return
            w2, w2b = self.load_w_cols(es, d["w_pw2"][i], 0, D, "w_pw2")
            NW = CONV_NW
            dg = Ring(nc, es, "dg", 2, [128, 31, 128], BF16)
            y32 = sb("y32", [128, KC, NW], F32); y32b = Buf("y32")
            ybf = sb("ybf", [128, KC, NW], BF16); ybfb = Buf("ybf")
            ysq = sb("ysq", [128, KC, NW], BF16); ysqb = Buf("ysq")
            yn = sb("yn", [128, KC, NW], BF16); ynb = Buf("yn")
            mean = sb("cmean", [128, NW], F32); meanb = Buf("cmean")
            m2 = sb("cm2", [128, NW], F32); m2b = Buf("cm2")
            rstd = sb("crstd", [128, NW], F32); rstdb = Buf("crstd")
            t1 = Ring(nc, es, "ct1", 1, [128, NW], F32)
            t2 = Ring(nc, es, "ct2", 1, [128, NW], F32)
            pcv = Ring(nc, es, "pcv", 2, [128, 512], F32, psum=True)
            pst = Ring(nc, es, "pst", 2, [128, 512], F32, psum=True)
            pp2 = Ring(nc, es, "pp2", 3, [128, 512], F32, psum=True)
            ctiles = [(NW * j, NW) for j in range(SEQ // NW)] + [(SEQ, NS)]
            for (t0, n) in ctiles:
                t = min(t0 // 512, 4)
                sample = (t0 == SEQ)
                for kc in range(KC):
                    dgt, dgb = dg.next()
                    for j in range(31):
                        if j % 3 != 2:
                            fw.op(fw.dve, lambda: nc.vector.tensor_scalar(out=dgt[:, j, :], in0=self.ident_b[:],
                                                                          scalar1=wdw[:, kc, j:j + 1], scalar2=None,
                                                                          op0=ALU.mult),
                                  reads=[self.cb, wdwb], writes=[dgb])
                        else:
                            fw.op(fw.act, lambda: nc.scalar.activation(out=dgt[:, j, :], in_=self.ident_b[:],
                                                                       func=AF.Copy, scale=wdw[:, kc, j:j + 1]),
                                  reads=[self.cb, wdwb], writes=[dgb])
                    p, pb_ = pcv.next()
                    for j in range(31):
                        if sample:
                            rhs = ugs[:, kc, :, j:j + 4]
                            rb_ = ugsb
                        else:
                            rhs = ugp[:, kc, t0 + j:t0 + j + n]
                            rb_ = ugpb
                        self.mm(p[:, :n], dgt[:, j, :], rhs, j == 0, j == 30, [dgb, rb_], [pb_])
                    bdw = prm[:, 16 + kc:17 + kc]
                    fw.op(fw.act, lambda: nc.scalar.activation(out=y32[:, kc, :n], in_=p[:, :n], func=AF.Identity, bias=bdw),
                          reads=[pb_, prmb], writes=[y32b])
                    fw.op(fw.act, lambda: nc.scalar.activation(out=ybf[:, kc, :n], in_=p[:, :n], func=AF.Identity, bias=bdw),
                          reads=[pb_, prmb], writes=[ybfb])
                    fw.op(fw.act, lambda: nc.scalar.activation(out=ysq[:, kc, :n], in_=p[:, :n], func=AF.Square, bias=bdw),
                          reads=[pb_, prmb], writes=[ysqb])
                pm, pmb = pst.next()
                for kc in range(KC):
                    self.mm(pm[:, :n], self.ones_bf[:], ybf[:, kc, :n], kc == 0, kc == KC - 1, [ybfb, self.cb], [pmb])
                pe2, pe2b = pst.next()
                for kc in range(KC):
                    self.mm(pe2[:, :n], self.ones_bf[:], ysq[:, kc, :n], kc == 0, kc == KC - 1, [ysqb, self.cb], [pe2b])
                fw.op(fw.act, lambda: nc.scalar.activation(out=mean[:, :n], in_=pm[:, :n], func=AF.Copy, scale=1.0 / D),
                      reads=[pmb], writes=[meanb])
                fw.op(fw.dve, lambda: nc.vector.tensor_tensor(out=m2[:, :n], in0=mean[:, :n], in1=mean[:, :n], op=ALU.mult),
                      reads=[meanb], writes=[m2b])
                fw.op(fw.dve, lambda: nc.vector.scalar_tensor_tensor(out=m2[:, :n], in0=pe2[:, :n], scalar=1.0 / D,
                                                                     in1=m2[:, :n], op0=ALU.mult, op1=ALU.subtract),
                      reads=[pe2b, m2b], writes=[m2b])
                fw.op(fw.act, lambda: nc.scalar.activation(out=m2[:, :n], in_=m2[:, :n], func=AF.Sqrt,
                                                           bias=self.epsT[:, 0:1]), reads=[m2b, self.cb], writes=[m2b])
                fw.op(fw.dve, lambda: nc.vector.reciprocal(out=rstd[:, :n], in_=m2[:, :n]), reads=[m2b], writes=[rstdb])
                for kc in range(KC):
                    a1, a1b = t1.next()
                    a2, a2b = t2.next()
                    fw.op(fw.dve, lambda: nc.vector.tensor_tensor(out=a1[:, :n], in0=y32[:, kc, :n], in1=mean[:, :n],
                                                                  op=ALU.subtract), reads=[y32b, meanb], writes=[a1b])
                    fw.op(fw.dve, lambda: nc.vector.tensor_tensor(out=a2[:, :n], in0=a1[:, :n], in1=rstd[:, :n],
                                                                  op=ALU.mult), reads=[a1b, rstdb], writes=[a2b])
                    fw.op(fw.act, lambda: nc.scalar.activation(out=yn[:, kc, :n], in_=a2[:, :n], func=AF.Silu,
                                                               scale=prm[:, 24 + kc:25 + kc], bias=prm[:, 32 + kc:33 + kc]),
                          reads=[a2b, prmb], writes=[ynb])
                for m in range(KC):
                    p, pb_ = pp2.next()
                    for kc in range(KC):
                        self.mm(p[:, :n], w2[:, kc, m * 128:(m + 1) * 128], yn[:, kc, :n], kc == 0, kc == KC - 1,
                                [w2b, ynb], [pb_])
                    fw.op(fw.dve, lambda: nc.vector.scalar_tensor_tensor(
                        out=self.x[:, m, t0:t0 + n], in0=p[:, :n], scalar=prm[:, 40 + m:41 + m],
                        in1=self.x[:, m, t0:t0 + n], op0=ALU.add, op1=ALU.add),
                          reads=[pb_, prmb, self.xb[t]], writes=[self.xb[t]])
            fw.barrier()

def _consts():
    inv = np.power(np.float32(10000.0), -np.arange(0, 32, 2, dtype=np.float32) / np.float32(32)).astype(np.float32)
    pos = np.concatenate([np.arange(SEQ), np.tile(PAST + np.arange(4), NSEQ)]).astype(np.float32)
    ang = (pos[:, None] * inv[None, :]).astype(np.float32)
    ang = np.concatenate([ang, ang], axis=-1)
    cos = np.cos(ang).astype(np.float32).T.copy()
    sin = np.sin(ang).astype(np.float32).T.copy()
    sin[0:16] = -sin[0:16]
    s_ = np.arange(64)
    mask4 = ((s_[:, None] // 4 == s_[None, :] // 4) & (s_[:, None] <= s_[None, :])).astype(np.float32)
    selb = (s_[:, None] // 4 == np.arange(NSEQ)[None, :]).astype(np.float32)
    maskq = np.zeros((64, NSEQ, 8, 4), np.float32)
    for b in range(NSEQ):
        for sl in range(4):
            maskq[4 * b + sl, b, :, sl:] = 1.0
    return cos, sin, mask4, maskq.reshape(64, NSEQ * 32), selb


def _prep(inp, cores=None):
    f32 = lambda a: np.ascontiguousarray(np.asarray(a), dtype=np.float32)
    cos, sin, mask4, maskq, selb = _consts()
    xp = f32(inp["x_prompt"])
    xs = f32(inp["x_sample"])
    shared = {
        "rope_cos": cos, "rope_sin": sin, "mask4": mask4, "maskq": maskq, "selb": selb,
        "norm_mix": f32(inp["norm_mix"]), "norm_mlp": f32(inp["norm_mlp"]),
        "norm_final": f32(inp["norm_final"]).reshape(1, D),
        "w_in": f32(inp["w_in"]), "w_gate_a2": f32(inp["w_gate_a2"]), "b_gate_a": f32(inp["b_gate_a"]),
        "gla_norm": f32(inp["gla_norm"]), "mla_q_norm": f32(inp["mla_q_norm"]),
        "mla_kv_norm": f32(inp["mla_kv_norm"]), "w_uq": f32(inp["w_uq"]), "w_ukv": f32(inp["w_ukv"]),
        "w_ukT": np.ascontiguousarray(f32(inp["w_ukv"]).reshape(2, 256, 8, 128)[:, :, :, :64].transpose(0, 3, 2, 1)),
        "w_out_ab": f32(inp["w_out_ab"]), "w_pw1": f32(inp["w_pw1"]), "b_pw1": f32(inp["b_pw1"]),
        "w_dwT": np.ascontiguousarray(f32(inp["w_dw"]).transpose(0, 2, 1)), "b_dw": f32(inp["b_dw"]),
        "conv_ln_g": f32(inp["conv_ln_g"]), "conv_ln_b": f32(inp["conv_ln_b"]),
        "w_pw2": f32(inp["w_pw2"]), "b_pw2": f32(inp["b_pw2"]),
        "w_up": f32(inp["w_up"]), "w_down": f32(inp["w_down"]),
    }
    if "nocache" not in DBG:
        shared["cache_kv"] = f32(inp["cache_kv"])
        shared["cache_kr"] = f32(inp["cache_kr"])
    pt = np.asarray(inp["page_table"]).astype(np.int32)
    sg = f32(inp["state_gla"])
    sc = f32(inp["state_conv"])
    in_maps = []
    for c in (range(NCORES) if cores is None else cores):
        sl = slice(NSEQ * c, NSEQ * (c + 1))
        m = dict(shared)
        m["xin"] = np.ascontiguousarray(np.concatenate([xp[c].T, xs[sl].reshape(NS, D).T], axis=1))
        m["page_table"] = np.ascontiguousarray(np.broadcast_to(pt[sl].reshape(1, NSEQ * NPAGES), (128, NSEQ * NPAGES)))
        m["iota_p"] = np.arange(128, dtype=np.float32).reshape(128, 1)
        m["state_gla"] = np.ascontiguousarray(sg[:, sl])
        m["state_conv"] = np.ascontiguousarray(sc[:, sl].transpose(0, 3, 1, 2))
        in_maps.append(m)
    return in_maps


def kernel(**inp):
    nc = Builder().build()
    in_maps = _prep(inp)
    res = run_bass_kernel_spmd(nc, in_maps, core_ids=list(range(NCORES)))
    return _assemble(res.results)


def _assemble(R):
    B = NCORES
    y_p = np.zeros((B, SEQ, D), np.float32)
    y_s = np.zeros((B * NSEQ, 4, D), np.float32)
    kv_p = np.zeros((2, B, SEQ, 256), np.float32)
    kr_p = np.zeros((2, B, SEQ, 32), np.float32)
    gla_p = np.zeros((2, B, 4, 64, 128), np.float32)
    conv_p = np.zeros((2, B, 30, D), np.float32)
    kv_s = np.zeros((2, B * NSEQ, 4, 256), np.float32)
    kr_s = np.zeros((2, B * NSEQ, 4, 32), np.float32)
    gla_s = np.zeros((2, B * NSEQ, 4, 64, 128), np.float32)
    conv_s = np.zeros((2, B * NSEQ, 30, D), np.float32)
    for c in range(B):
        r = R[c]
        sl = slice(NSEQ * c, NSEQ * (c + 1))
        y = np.asarray(r["y"])
        y_p[c] = y[:, :SEQ].T
        y_s[sl] = y[:, SEQ:].T.reshape(NSEQ, 4, D)
        kv = np.asarray(r["kv"]); kr = np.asarray(r["kr"])
        for i in range(2):
            kv_p[i, c] = kv[i, :, :SEQ].T
            kv_s[i, sl] = kv[i, :, SEQ:].T.reshape(NSEQ, 4, 256)
            kr_p[i, c] = kr[i, :, :SEQ].T
            kr_s[i, sl] = kr[i, :, SEQ:].T.reshape(NSEQ, 4, 32)
            conv_p[i, c] = np.asarray(r["conv_p"])[i].T
            conv_s[i, sl] = np.asarray(r["conv_s"])[i].transpose(1, 2, 0)
        gla_p[:, c] = np.asarray(r["gla_p"])
        gla_s[:, sl] = np.asarray(r["gla_s"])
    return (y_p, y_s, kv_p, kr_p, gla_p, conv_p, kv_s, kr_s, gla_s, conv_s)
```
